# Optimizing a Trainium2 kernel written in Bass

```python
import math
import jax, jax.numpy as jnp
from jax import lax
import numpy as np


D_MODEL = 1024
BATCH = 4
SEQ = 4096
DEPTH = 2

D_MIX = 2 * D_MODEL
N_GROUPS_MIX = 4
GROUP_W = D_MIX // N_GROUPS_MIX
HEAD_DIM = 64
Q_BLOCK = 128
EPS = 1e-6
NEG_INF = -1e30
TINY = 1e-30

FOX_HEADS = GROUP_W // HEAD_DIM
FOX_F_BIAS_INIT = 2.0

SSM_HEADS = GROUP_W // HEAD_DIM
SSM_STATE = 128
SSM_GROUPS = 2
SSM_CONV = 4
SSM_CHUNK = 128
SSM_CONV_DIM = GROUP_W + 2 * SSM_GROUPS * SSM_STATE
DT_MIN = 1e-3
DT_MAX = 1e-1

NSA_HEADS = GROUP_W // HEAD_DIM
NSA_KV_HEADS = 2
NSA_REP = NSA_HEADS // NSA_KV_HEADS
NSA_KV_W = NSA_KV_HEADS * HEAD_DIM
CMP_BLOCK = 32
CMP_STRIDE = 16
CMP_HIDDEN = 2 * HEAD_DIM
SEL_BLOCK = 64
SEL_TOPK = 16
WINDOW = 512
SEL_FORCE = 1e9

MEM_TOKENS = 256
MEM_HEADS = 4
MEM_HEAD_DIM = GROUP_W // MEM_HEADS

REL_BUCKETS = 32
REL_MAX_DIST = 128

FOX_COLS = 4 * GROUP_W + FOX_HEADS
SSM_COLS = GROUP_W + SSM_CONV_DIM + SSM_HEADS
NSA_COLS = 2 * GROUP_W + 6 * NSA_KV_W + 3 * NSA_HEADS
MEM_COLS = 2 * GROUP_W
IN_COLS = FOX_COLS + SSM_COLS + NSA_COLS + MEM_COLS

kernel_name = 'hybrid_fox_ssd_nsa_mem_block'


def _rmsnorm(x, g):
    xf = x.astype(jnp.float32)
    y = xf * lax.rsqrt(jnp.mean(xf * xf, axis=-1, keepdims=True) + EPS)
    return y.astype(x.dtype) * g


def _split(x, sizes):
    offs = [int(o) for o in np.cumsum(sizes)[:-1]]
    return jnp.split(x, offs, axis=-1)


def _t5_bucket(dist):
    n = jnp.maximum(dist, 0)
    max_exact = REL_BUCKETS // 2
    nf = jnp.maximum(n, 1).astype(jnp.float32)
    large = max_exact + (jnp.log(nf / max_exact) / math.log(REL_MAX_DIST / max_exact)
                         * (REL_BUCKETS - max_exact)).astype(jnp.int32)
    large = jnp.minimum(large, REL_BUCKETS - 1)
    return jnp.where(n < max_exact, n, large)


def _masked_softmax(s, mask):
    s = jnp.where(mask, s.astype(jnp.float32), NEG_INF)
    m = jnp.max(s, axis=-1, keepdims=True)
    e = jnp.where(mask, jnp.exp(s - m), 0.0)
    return e / jnp.maximum(jnp.sum(e, axis=-1, keepdims=True), TINY)


def _fox_mixer(p, f_bias):
    B, S, _ = p.shape
    q, k, v, gate, f_logit = _split(p, [GROUP_W, GROUP_W, GROUP_W, GROUP_W, FOX_HEADS])
    q = q.reshape(B, S, FOX_HEADS, HEAD_DIM) * HEAD_DIM ** -0.5
    k = k.reshape(B, S, FOX_HEADS, HEAD_DIM)
    v = v.reshape(B, S, FOX_HEADS, HEAD_DIM)
    log_f = jax.nn.log_sigmoid((f_logit + f_bias).astype(jnp.float32))
    c = jnp.cumsum(log_f, axis=1).transpose(0, 2, 1)
    kpos = jnp.arange(S)

    def block(i):
        t0 = i * Q_BLOCK
        qb = lax.dynamic_slice_in_dim(q, t0, Q_BLOCK, axis=1)
        cb = lax.dynamic_slice_in_dim(c, t0, Q_BLOCK, axis=2)
        s = jnp.einsum('bqhd,bkhd->bhqk', qb, k).astype(jnp.float32)
        s = s + cb[..., :, None] - c[..., None, :]
        qpos = t0 + jnp.arange(Q_BLOCK)
        pr = _masked_softmax(s, kpos[None, :] <= qpos[:, None])
        return jnp.einsum('bhqk,bkhd->bqhd', pr.astype(v.dtype), v)

    o = lax.map(block, jnp.arange(S // Q_BLOCK))
    o = o.transpose(1, 0, 2, 3, 4).reshape(B, S, GROUP_W)
    return o * jax.nn.silu(gate)


def _causal_dwconv(x, w, b):
    C = x.shape[-1]
    y = lax.conv_general_dilated(x, w[:, None, :], window_strides=(1,),
                                 padding=[(SSM_CONV - 1, 0)],
                                 dimension_numbers=('NWC', 'WIO', 'NWC'),
                                 feature_group_count=C)
    return y + b


def _ssd_chunked(xdt, dA, Bh, Ch):
    B, S, H, P = xdt.shape
    N = Bh.shape[-1]
    nc = S // SSM_CHUNK
    x = xdt.reshape(B, nc, SSM_CHUNK, H, P)
    Bc = Bh.reshape(B, nc, SSM_CHUNK, H, N)
    Cc = Ch.reshape(B, nc, SSM_CHUNK, H, N)
    a = dA.reshape(B, nc, SSM_CHUNK, H).transpose(0, 3, 1, 2)
    acs = jnp.cumsum(a, axis=-1)
    idx = jnp.arange(SSM_CHUNK)
    causal = idx[:, None] >= idx[None, :]
    seg = jnp.exp(jnp.where(causal, acs[..., :, None] - acs[..., None, :], -jnp.inf))
    cb = jnp.einsum('bclhn,bcshn->bhcls', Cc, Bc) * seg
    y_diag = jnp.einsum('bhcls,bcshp->bclhp', cb, x)
    decay_to_end = jnp.exp(acs[..., -1:] - acs)
    chunk_states = jnp.einsum('bclhn,bhcl,bclhp->bchpn', Bc, decay_to_end, x)
    chunk_decay = jnp.exp(acs[..., -1])

    def step(state, inp):
        st, dec = inp
        return state * dec[:, :, None, None] + st, state

    init = jnp.zeros((B, H, P, N), chunk_states.dtype)
    _, prev = lax.scan(step, init, (jnp.moveaxis(chunk_states, 1, 0), jnp.moveaxis(chunk_decay, 2, 0)))
    prev = jnp.moveaxis(prev, 0, 1)
    y_off = jnp.einsum('bclhn,bchpn,bhcl->bclhp', Cc, prev, jnp.exp(acs))
    return (y_diag + y_off).reshape(B, S, H, P)


def _ssd_mixer(p, conv_w, conv_b, dt_bias, a_log, d_skip, norm_g):
    B, S, _ = p.shape
    z, xbc, dt_raw = _split(p, [GROUP_W, SSM_CONV_DIM, SSM_HEADS])
    xbc = jax.nn.silu(_causal_dwconv(xbc, conv_w, conv_b))
    xs, bm, cm = _split(xbc, [GROUP_W, SSM_GROUPS * SSM_STATE, SSM_GROUPS * SSM_STATE])
    xs = xs.reshape(B, S, SSM_HEADS, HEAD_DIM)
    rep = SSM_HEADS // SSM_GROUPS
    bh = jnp.repeat(bm.reshape(B, S, SSM_GROUPS, SSM_STATE), rep, axis=2)
    ch = jnp.repeat(cm.reshape(B, S, SSM_GROUPS, SSM_STATE), rep, axis=2)
    dt = jax.nn.softplus((dt_raw + dt_bias).astype(jnp.float32))
    a = -jnp.exp(a_log.astype(jnp.float32))
    y = _ssd_chunked(xs * dt[..., None], dt * a, bh, ch)
    y = (y + xs * d_skip[:, None]).reshape(B, S, GROUP_W)
    yg = (y * jax.nn.silu(z)).reshape(B, S, SSM_GROUPS, GROUP_W // SSM_GROUPS)
    return _rmsnorm(yg, norm_g.reshape(SSM_GROUPS, -1)).reshape(B, S, GROUP_W)


def _nsa_mixer(p, cmp_pe, cmp_w1, cmp_w2, rel_table):
    B, S, _ = p.shape
    G, R = NSA_KV_HEADS, NSA_REP
    q, kc, vc, ks, vs, kw, vw, g_logit, gate = _split(
        p, [GROUP_W] + [NSA_KV_W] * 6 + [3 * NSA_HEADS, GROUP_W])
    q = q.reshape(B, S, G, R, HEAD_DIM).transpose(0, 2, 3, 1, 4) * HEAD_DIM ** -0.5
    kc, vc, ks, vs, kw, vw = [t.reshape(B, S, G, HEAD_DIM) for t in (kc, vc, ks, vs, kw, vw)]
    gates = jax.nn.sigmoid(g_logit.astype(jnp.float32)).reshape(B, S, 3, G, R)

    n_cmp = (S - CMP_BLOCK) // CMP_STRIDE + 1
    cidx = jnp.arange(n_cmp)[:, None] * CMP_STRIDE + jnp.arange(CMP_BLOCK)[None, :]

    def compress(t, j):
        blk = t[:, cidx] + cmp_pe[j][None, None, :, None, :]
        flat = blk.transpose(0, 1, 3, 2, 4).reshape(B, n_cmp, G, CMP_BLOCK * HEAD_DIM)
        return jax.nn.silu(flat @ cmp_w1[j]) @ cmp_w2[j]

    k_cmp = compress(kc, 0)
    v_cmp = compress(vc, 1)
    cmp_start = cidx[:, 0]
    cmp_end = cidx[:, -1]

    n_sel = S // SEL_BLOCK
    top_n = min(SEL_TOPK, n_sel)
    sel_start = jnp.arange(n_sel) * SEL_BLOCK
    overlap = ((cmp_start[:, None] < sel_start[None, :] + SEL_BLOCK)
               & (cmp_start[:, None] + CMP_BLOCK > sel_start[None, :])).astype(jnp.float32)
    ks_blk = ks.reshape(B, n_sel, SEL_BLOCK, G, HEAD_DIM).transpose(0, 3, 1, 2, 4)
    vs_blk = vs.reshape(B, n_sel, SEL_BLOCK, G, HEAD_DIM).transpose(0, 3, 1, 2, 4)
    kw_pad = jnp.pad(kw, ((0, 0), (WINDOW, 0), (0, 0), (0, 0)))
    vw_pad = jnp.pad(vw, ((0, 0), (WINDOW, 0), (0, 0), (0, 0)))
    table_g = rel_table.reshape(REL_BUCKETS, G, R)
    bidx = jnp.arange(B)[:, None, None, None]
    gidx = jnp.arange(G)[None, :, None, None]
    sel_off = jnp.arange(SEL_BLOCK)
    win_off = jnp.arange(WINDOW + Q_BLOCK)
    sel_j = jnp.arange(n_sel)[None, :]

    def head_bias(dist):
        return rel_table[_t5_bucket(dist)].reshape(dist.shape + (G, R)).transpose(2, 3, 0, 1)

    def block(i):
        t0 = i * Q_BLOCK
        qpos = t0 + jnp.arange(Q_BLOCK)
        qb = lax.dynamic_slice_in_dim(q, t0, Q_BLOCK, axis=3)
        dist_c = qpos[:, None] - cmp_end[None, :]
        s_c = jnp.einsum('bgrqd,bcgd->bgrqc', qb, k_cmp).astype(jnp.float32) + head_bias(dist_c)
        p_c = _masked_softmax(s_c, dist_c >= 0)
        o_c = jnp.einsum('bgrqc,bcgd->bgrqd', p_c.astype(v_cmp.dtype), v_cmp)
        imp = jnp.einsum('bgrqc,cj->bgqj', p_c, overlap)
        cur = qpos[:, None] // SEL_BLOCK
        forced = (sel_j == 0) | (sel_j == cur) | (sel_j == cur - 1)
        imp = jnp.where(sel_j <= cur, jnp.where(forced, SEL_FORCE, imp), -SEL_FORCE)
        _, top = lax.top_k(imp, top_n)
        k_sel = ks_blk[bidx, gidx, top].reshape(B, G, Q_BLOCK, top_n * SEL_BLOCK, HEAD_DIM)
        v_sel = vs_blk[bidx, gidx, top].reshape(B, G, Q_BLOCK, top_n * SEL_BLOCK, HEAD_DIM)
        pos_s = (top[..., None] * SEL_BLOCK + sel_off).reshape(B, G, Q_BLOCK, top_n * SEL_BLOCK)
        dist_s = qpos[None, None, :, None] - pos_s
        bias_s = table_g[_t5_bucket(dist_s), gidx].transpose(0, 1, 4, 2, 3)
        s_s = jnp.einsum('bgrqd,bgqtd->bgrqt', qb, k_sel).astype(jnp.float32) + bias_s
        p_s = _masked_softmax(s_s, (dist_s >= 0)[:, :, None])
        o_s = jnp.einsum('bgrqt,bgqtd->bgrqd', p_s.astype(v_sel.dtype), v_sel)
        kwb = lax.dynamic_slice_in_dim(kw_pad, t0, WINDOW + Q_BLOCK, axis=1)
        vwb = lax.dynamic_slice_in_dim(vw_pad, t0, WINDOW + Q_BLOCK, axis=1)
        kpos_w = t0 - WINDOW + win_off
        dist_w = qpos[:, None] - kpos_w[None, :]
        s_w = jnp.einsum('bgrqd,btgd->bgrqt', qb, kwb).astype(jnp.float32) + head_bias(dist_w)
        p_w = _masked_softmax(s_w, (dist_w >= 0) & (dist_w < WINDOW) & (kpos_w[None, :] >= 0))
        o_w = jnp.einsum('bgrqt,btgd->bgrqd', p_w.astype(vwb.dtype), vwb)
        gb = lax.dynamic_slice_in_dim(gates, t0, Q_BLOCK, axis=1).transpose(2, 0, 3, 4, 1)[..., None]
        o = gb[0] * o_c + gb[1] * o_s + gb[2] * o_w
        return o.transpose(0, 3, 1, 2, 4).reshape(B, Q_BLOCK, GROUP_W)

    o = lax.map(block, jnp.arange(S // Q_BLOCK))
    o = o.transpose(1, 0, 2, 3).reshape(B, S, GROUP_W)
    return o * jax.nn.silu(gate)


def _mem_mixer(p, mem_kv):
    B, S, _ = p.shape
    M = mem_kv.shape[1]
    q, gate = _split(p, [GROUP_W, GROUP_W])
    q = q.reshape(B, S, MEM_HEADS, MEM_HEAD_DIM) * MEM_HEAD_DIM ** -0.5
    k, v = _split(mem_kv, [GROUP_W, GROUP_W])
    k = k.reshape(B, M, MEM_HEADS, MEM_HEAD_DIM)
    v = v.reshape(B, M, MEM_HEADS, MEM_HEAD_DIM)
    s = jnp.einsum('bshd,bmhd->bhsm', q, k).astype(jnp.float32)
    pr = jax.nn.softmax(s, axis=-1).astype(v.dtype)
    o = jnp.einsum('bhsm,bmhd->bshd', pr, v).reshape(B, S, GROUP_W)
    return o * jax.nn.silu(gate)


def setup_inputs(seed: int = 0) -> dict:
    key = jax.random.key(seed)
    ks = jax.random.split(key, 20)
    f32 = jnp.float32

    def nrm(k, shape, scale):
        return scale * jax.random.normal(k, shape, f32)

    dt = jnp.exp(jax.random.uniform(ks[7], (DEPTH, SSM_HEADS), f32, math.log(DT_MIN), math.log(DT_MAX)))
    return {
        'x': nrm(ks[0], (BATCH, SEQ, D_MODEL), 1.0),
        'mem': nrm(ks[1], (BATCH, MEM_TOKENS, D_MODEL), 1.0),
        'norm_g': 1.0 + nrm(ks[2], (DEPTH, D_MODEL), 0.02),
        'w_in': nrm(ks[3], (DEPTH, D_MODEL, IN_COLS), D_MODEL ** -0.5),
        'fox_f_bias': FOX_F_BIAS_INIT + nrm(ks[4], (DEPTH, FOX_HEADS), 0.5),
        'ssm_conv_w': nrm(ks[5], (DEPTH, SSM_CONV, SSM_CONV_DIM), SSM_CONV ** -0.5),
        'ssm_conv_b': nrm(ks[6], (DEPTH, SSM_CONV_DIM), 0.02),
        'ssm_dt_bias': dt + jnp.log(-jnp.expm1(-dt)),
        'ssm_a_log': jnp.log(jax.random.uniform(ks[8], (DEPTH, SSM_HEADS), f32, 1.0, 16.0)),
        'ssm_d': 1.0 + nrm(ks[9], (DEPTH, SSM_HEADS), 0.1),
        'ssm_norm_g': 1.0 + nrm(ks[10], (DEPTH, GROUP_W), 0.02),
        'nsa_cmp_pe': nrm(ks[11], (DEPTH, 2, CMP_BLOCK, HEAD_DIM), 0.02),
        'nsa_cmp_w1': nrm(ks[12], (DEPTH, 2, CMP_BLOCK * HEAD_DIM, CMP_HIDDEN), (CMP_BLOCK * HEAD_DIM) ** -0.5),
        'nsa_cmp_w2': nrm(ks[13], (DEPTH, 2, CMP_HIDDEN, HEAD_DIM), CMP_HIDDEN ** -0.5),
        'rel_bias_table': nrm(ks[14], (REL_BUCKETS, NSA_HEADS), 0.5),
        'mem_norm_g': 1.0 + nrm(ks[15], (DEPTH, D_MODEL), 0.02),
        'w_mem_kv': nrm(ks[16], (DEPTH, D_MODEL, 2 * GROUP_W), D_MODEL ** -0.5),
        'w_out': nrm(ks[17], (DEPTH, D_MIX, D_MODEL), D_MIX ** -0.5),
        'final_norm_g': 1.0 + nrm(ks[18], (D_MODEL,), 0.02),
    }


def reference(x, mem, norm_g, w_in, fox_f_bias, ssm_conv_w, ssm_conv_b, ssm_dt_bias, ssm_a_log,
              ssm_d, ssm_norm_g, nsa_cmp_pe, nsa_cmp_w1, nsa_cmp_w2, rel_bias_table, mem_norm_g,
              w_mem_kv, w_out, final_norm_g):
    for l in range(DEPTH):
        h = _rmsnorm(x, norm_g[l])
        p_fox, p_ssm, p_nsa, p_mem = _split(h @ w_in[l], [FOX_COLS, SSM_COLS, NSA_COLS, MEM_COLS])
        mem_kv = _rmsnorm(mem, mem_norm_g[l]) @ w_mem_kv[l]
        o = jnp.concatenate([
            _fox_mixer(p_fox, fox_f_bias[l]),
            _ssd_mixer(p_ssm, ssm_conv_w[l], ssm_conv_b[l], ssm_dt_bias[l], ssm_a_log[l],
                       ssm_d[l], ssm_norm_g[l]),
            _nsa_mixer(p_nsa, nsa_cmp_pe[l], nsa_cmp_w1[l], nsa_cmp_w2[l], rel_bias_table),
            _mem_mixer(p_mem, mem_kv),
        ], axis=-1)
        x = x + o @ w_out[l]
    return _rmsnorm(x, final_norm_g)
```

```python
import contextlib
import math
import numpy as np
import ml_dtypes
import concourse.bass as bass
import concourse.mybir as mybir
from concourse.bass_utils import run_bass_kernel_spmd

F32 = mybir.dt.float32
BF16 = mybir.dt.bfloat16
AF = mybir.ActivationFunctionType
ALU = mybir.AluOpType
AX = mybir.AxisListType
ENGS = ("pe", "act", "dve", "pool", "sp")
NEG = -30000.0
D_MODEL = 1024
IN_COLS = 6440
EPS = 1e-6


class Prog:
    def __init__(self, nc):
        self.nc = nc
        self.es = contextlib.ExitStack()
        self.streams = {e: [] for e in ENGS}
        self.cnt = {}
        self.sems = {}
        self.seen = {e: {} for e in ENGS}
        self.bufs = {}
        self.pools = {}
        for e in ENGS:
            self._newsem(e)

    def _newsem(self, key):
        self.sems[key] = self.es.enter_context(self.nc.semaphore("s_" + key))
        self.cnt[key] = 0

    def sb(self, name, shape, dtype, stack=None):
        self.uid = getattr(self, "uid", 0) + 1
        return (stack or self.es).enter_context(self.nc.sbuf_tensor("%s_u%d" % (name, self.uid), list(shape), dtype))

    def ps(self, name, shape, dtype=F32):
        return self.es.enter_context(self.nc.psum_tensor(name, list(shape), dtype))

    def _need(self, eng, reads, writes):
        need = {}

        def add(ev, raw):
            if ev is None:
                return
            k, v = ev
            if k == eng and (eng == "pe" or not raw):
                return
            if need.get(k, 0) < v:
                need[k] = v

        for b in reads:
            st = self.bufs.get(b)
            if st:
                add(st[0], True)
        for b in writes:
            st = self.bufs.get(b)
            if st:
                add(st[0], False)
                for ev in st[1]:
                    add(ev, False)
        for k, v in need.items():
            if self.seen[eng].get(k, 0) < v:
                self.seen[eng][k] = v
                self.streams[eng].append(("wait", k, v))

    def _mark(self, ev, reads, writes):
        for b in reads:
            st = self.bufs.setdefault(b, [None, []])
            st[1].append(ev)
        for b in writes:
            self.bufs[b] = [ev, []]

    def op(self, eng, fn, reads=(), writes=()):
        self._need(eng, reads, writes)
        self.cnt[eng] += 1
        ev = (eng, self.cnt[eng])
        self.streams[eng].append(("op", fn, eng, 1))
        self._mark(ev, reads, writes)
        return ev

    def dma(self, eng, fn, reads=(), writes=(), pool="d", npool=6):
        pl = self.pools.setdefault(pool, {"n": 0, "keys": []})
        i = pl["n"] % npool
        pl["n"] += 1
        if i >= len(pl["keys"]):
            key = "dma_%s_%d" % (pool, i)
            self._newsem(key)
            pl["keys"].append(key)
        key = pl["keys"][i]
        prev = self.cnt[key]
        if prev > 0 and self.seen[eng].get(key, 0) < prev:
            self.seen[eng][key] = prev
            self.streams[eng].append(("wait", key, prev))
        self._need(eng, reads, writes)
        self.cnt[key] += 16
        ev = (key, self.cnt[key])
        self.streams[eng].append(("op", fn, key, 16))
        self._mark(ev, reads, writes)
        return ev

    def wait_all(self, eng):
        for k, v in self.cnt.items():
            if k != eng and v > 0 and self.seen[eng].get(k, 0) < v:
                self.seen[eng][k] = v
                self.streams[eng].append(("wait", k, v))

    def barrier(self):
        for e in ENGS:
            self.wait_all(e)
        self.bufs = {}

    def ninstr(self):
        return sum(1 for e in ENGS for it in self.streams[e] if it[0] == "op")

    def emit(self):
        nc = self.nc
        sems = self.sems
        streams = self.streams

        def run(engobj, name):
            for it in streams[name]:
                if it[0] == "wait":
                    engobj.wait_ge(sems[it[1]], it[2])
                else:
                    ins = it[1](engobj)
                    ins.then_inc(sems[it[2]], it[3])

        with nc.Block() as block:
            @block.tensor
            def _(e):
                run(e, "pe")

            @block.scalar
            def _(e):
                run(e, "act")

            @block.vector
            def _(e):
                run(e, "dve")

            @block.gpsimd
            def _(e):
                run(e, "pool")

            @block.sync
            def _(e):
                run(e, "sp")

    def close(self):
        self.es.close()


FG = ["fq0", "fq1", "fk0", "fk1", "fg0", "fg1", "sz0", "sz1", "sx0", "sx1", "sB", "sC",
      "small", "nq0", "nq1", "ncv", "nkk", "ng0", "ng1", "mq0", "mq1", "mg0", "mg1"]
PFB_NAMES = ["fq0", "fq1", "fk0", "fk1", "fg0", "fg1", "sz0", "sz1", "nq0", "nq1", "ncv", "nkk",
             "ng0", "ng1", "mq0", "mq1", "mg0", "mg1"]
PFF_NAMES = ["sx0", "sx1", "sB", "sC", "small"]
PFB_ROW = {n: 128 * i for i, n in enumerate(PFB_NAMES)}
PFF_ROW = {n: 128 * i for i, n in enumerate(PFF_NAMES)}
NFCOL = 22 * 128 + 20
NTCOL = 384


def core_cols(hf):
    r = np.arange
    FOX, SSM, NSA, MEM = 0, 2056, 3600, 5416
    c = {}
    c["fq"] = FOX + 256 * hf + r(256)
    c["fk"] = FOX + 512 + 256 * hf + r(256)
    c["fv"] = FOX + 1024 + 256 * hf + r(256)
    c["fg"] = FOX + 1536 + 256 * hf + r(256)
    c["ff"] = FOX + 2048 + 4 * hf + r(4)
    c["sz"] = SSM + 256 * hf + r(256)
    c["sx"] = SSM + 512 + 256 * hf + r(256)
    c["sB"] = SSM + 1024 + 128 * hf + r(128)
    c["sC"] = SSM + 1280 + 128 * hf + r(128)
    c["sdt"] = SSM + 1536 + 4 * hf + r(4)
    c["nq"] = NSA + 256 * hf + r(256)
    c["kc"] = NSA + 512 + 64 * hf + r(64)
    c["vc"] = NSA + 640 + 64 * hf + r(64)
    c["ks"] = NSA + 768 + 64 * hf + r(64)
    c["vs"] = NSA + 896 + 64 * hf + r(64)
    c["kw"] = NSA + 1024 + 64 * hf + r(64)
    c["vw"] = NSA + 1152 + 64 * hf + r(64)
    c["gl"] = NSA + 1280 + np.array([br * 8 + hf * 4 + rr for br in range(3) for rr in range(4)])
    c["ng"] = NSA + 1304 + 256 * hf + r(256)
    c["mq"] = MEM + 256 * hf + r(256)
    c["mg"] = MEM + 512 + 256 * hf + r(256)
    order = ["fq", "fk", "fg", "sz", "sx", "sB", "sC", "ff", "sdt", "gl", "nq", "kc", "vc", "ks", "kw",
             "ng", "mq", "mg", "fv", "vs", "vw"]
    return np.concatenate([c[k] for k in order])


def t5_bucket(d):
    d = np.asarray(d)
    n = np.maximum(d, 0)
    nf = np.maximum(n, 1).astype(np.float32)
    large = 16 + (np.log(nf / 16) / math.log(128 / 16) * 16).astype(np.int32)
    large = np.minimum(large, 31)
    return np.where(n < 16, n, large)


class Ctx:
    pass


def mm(P, out, lhsT, rhs, start, stop, reads, writes):
    P.op("pe", lambda e: e.matmul(out, lhsT=lhsT, rhs=rhs, start=start, stop=stop), reads=reads, writes=writes)


def phase_inproj(P, C, S, xT, wl, gl, PFb, PFf, VT):
    nc = P.nc
    NQC = S // 512
    with contextlib.ExitStack() as st:
        hT = P.sb("hT", [128, 8, S], BF16, st)
        xs = [P.sb("xst%d" % i, [128, 8, 512], F32, st) for i in range(2)]
        sq = P.sb("sq", [128, 8, 512], BF16, st)
        rt = P.sb("rt", [128, 512], F32, st)
        gcol = P.sb("gcol", [128, 8], F32, st)
        wst = [P.sb("wst%d" % i, [128, 8, 128], F32, st) for i in range(2)]
        wbf = [P.sb("wbf%d" % i, [128, 8, 128], BF16, st) for i in range(2)]
        wtst = P.sb("wtst", [128, 8, NTCOL], F32, st)
        wtb = P.sb("wtb", [128, 8, NTCOL], BF16, st)
        stb = [P.sb("stb%d" % i, [128, S], BF16, st) for i in range(2)]
        stf = P.sb("stf", [128, S], F32, st)
        vst = [P.sb("vst%d" % i, [128, 6, 128], BF16, st) for i in range(2)]
        P.dma("sp", lambda e: e.dma_start(out=gcol[:], in_=gl), writes=["gcol"])
        for i in range(2):
            P.op("pool", lambda e, i=i: e.memset(vst[i][:], 1.0), writes=["vst%d" % i])
        xv = xT.rearrange("(k p) t -> p k t", p=128)
        for qc in range(NQC):
            xb = xs[qc % 2]
            xk = "xst%d" % (qc % 2)
            P.dma("sp", lambda e, xb=xb, qc=qc: e.dma_start(out=xb[:], in_=xv[:, :, qc * 512:(qc + 1) * 512]),
                  writes=[xk], pool="x", npool=2)
            P.op("act", lambda e, xb=xb: e.activation(out=sq[:], in_=xb[:], func=AF.Square), reads=[xk], writes=["sq"])
            ps = C.psb[qc % 2]
            pk = "psb%d" % (qc % 2)
            for k in range(8):
                mm(P, ps[:], C.onesb[:, 0:128], sq[:, k, :], k == 0, k == 7, ["sq", "onesb"], [pk])
            P.op("act", lambda e, ps=ps: e.activation(out=rt[:], in_=ps[:], func=AF.Sqrt, bias=C.epsc[:, 0:1], scale=1.0 / D_MODEL),
                 reads=[pk, "epsc"], writes=["rt"])
            P.op("dve", lambda e: e.reciprocal(out=rt[:], in_=rt[:]), reads=["rt"], writes=["rt"])
            for k in range(8):
                P.op("dve",
                     lambda e, xb=xb, k=k, qc=qc: e.scalar_tensor_tensor(
                         out=hT[:, k, qc * 512:(qc + 1) * 512], in0=xb[:, k, :], scalar=gcol[:, k:k + 1], in1=rt[:],
                         op0=ALU.mult, op1=ALU.mult),
                     reads=[xk, "rt", "gcol"], writes=["hT"])
        wv = wl.rearrange("(k p) c -> p k c", p=128)
        col = 0
        nb = 0
        for gi, gname in enumerate(FG):
            ncol = 20 if gname == "small" else 128
            ws = wst[gi % 2]
            wb = wbf[gi % 2]
            wk, bk = "wst%d" % (gi % 2), "wbf%d" % (gi % 2)
            P.dma("sp", lambda e, ws=ws, col=col, ncol=ncol: e.dma_start(out=ws[:, :, 0:ncol], in_=wv[:, :, col:col + ncol]),
                  writes=[wk], pool="w", npool=2)
            P.op("pool", lambda e, ws=ws, wb=wb, ncol=ncol: e.tensor_copy(out=wb[:, :, 0:ncol], in_=ws[:, :, 0:ncol]),
                 reads=[wk], writes=[bk])
            isf = gname in PFF_ROW
            if isf:
                stg, sk = stf, "stf"
            else:
                stg, sk = stb[nb % 2], "stb%d" % (nb % 2)
                nb += 1
            for qc in range(NQC):
                ps = C.psb[(gi * NQC + qc) % 2]
                pk = "psb%d" % ((gi * NQC + qc) % 2)
                for k in range(8):
                    mm(P, ps[0:ncol, :], wb[:, k, 0:ncol], hT[:, k, qc * 512:(qc + 1) * 512], k == 0, k == 7, [bk, "hT"], [pk])
                o = stg[0:ncol, qc * 512:(qc + 1) * 512]
                pin = ps[0:ncol, :]
                base = gname[:2]
                eng = "act"
                if base in ("fg", "sz", "ng", "mg"):
                    fn = lambda e, o=o, pin=pin: e.activation(out=o, in_=pin, func=AF.Silu)
                elif base in ("fq", "nq"):
                    fn = lambda e, o=o, pin=pin: e.activation(out=o, in_=pin, func=AF.Copy, scale=0.125)
                elif base == "mq":
                    fn = lambda e, o=o, pin=pin: e.activation(out=o, in_=pin, func=AF.Copy, scale=128.0 ** -0.5)
                else:
                    eng = "dve"
                    fn = lambda e, o=o, pin=pin: e.tensor_copy(out=o, in_=pin)
                P.op(eng, fn, reads=[pk], writes=[sk])
            if isf:
                r0 = PFF_ROW[gname]
                P.dma("sp", lambda e, stg=stg, r0=r0, ncol=ncol: e.dma_start(out=PFf[r0:r0 + ncol, :], in_=stg[0:ncol, :]),
                      reads=[sk], writes=["PFf"], pool="o", npool=4)
            else:
                r0 = PFB_ROW[gname]
                P.dma("sp", lambda e, stg=stg, r0=r0: e.dma_start(out=PFb[r0:r0 + 128, :], in_=stg[:, :]),
                      reads=[sk], writes=["PFb"], pool="o", npool=4)
            col += ncol
        P.dma("sp", lambda e: e.dma_start(out=wtst[:], in_=wv[:, :, col:col + NTCOL]), writes=["wtst"], pool="w", npool=2)
        P.op("pool", lambda e: e.tensor_copy(out=wtb[:], in_=wtst[:]), reads=["wtst"], writes=["wtb"])
        for tt in range(S // 128):
            ps = C.psb[tt % 2]
            pk = "psb%d" % (tt % 2)
            for k in range(8):
                mm(P, ps[:, 0:NTCOL], hT[:, k, tt * 128:(tt + 1) * 128], wtb[:, k, :], k == 0, k == 7, ["wtb", "hT"], [pk])
            vs_ = vst[tt % 2]
            vk = "vst%d" % (tt % 2)
            P.op("act" if tt % 2 == 0 else "dve",
                 (lambda e, vs_=vs_, ps=ps: e.activation(out=vs_[:, :, 0:64], in_=ps[:, 0:NTCOL].rearrange("p (a b) -> p a b", b=64), func=AF.Copy))
                 if tt % 2 == 0 else
                 (lambda e, vs_=vs_, ps=ps: e.tensor_copy(out=vs_[:, :, 0:64], in_=ps[:, 0:NTCOL].rearrange("p (a b) -> p a b", b=64))),
                 reads=[pk], writes=[vk])
            P.dma("sp", lambda e, vs_=vs_, tt=tt: e.dma_start(out=VT[tt * 128:(tt + 1) * 128, :, :], in_=vs_[:]),
                  reads=[vk], writes=["VT"], pool="o", npool=4)
    P.barrier()


def attn_tiles(P, C, tiles, O, okey, N):
    nt = len(tiles)
    i = 0
    while i < nt:
        grp = tiles[i:i + 2]
        sp = C.spair[C.spi % 2]
        spk = "spair%d" % (C.spi % 2)
        pt = C.ptile[C.spi % 2]
        ptk = "ptile%d" % (C.spi % 2)
        C.spi += 1
        for j, t in enumerate(grp):
            o = sp[:, j * 512:j * 512 + N]
            ex = t.get("extra")
            mm(P, o, t["lhsT"], t["rhs"], True, ex is None, t["rk"], [spk])
            if ex is not None:
                mm(P, o, ex[0], ex[1], False, True, ex[2], [spk])
        ng = len(grp)
        if N == 512:
            P.op("act", lambda e, sp=sp, pt=pt, ng=ng: e.activation(out=pt[:, 0:ng * 512], in_=sp[:, 0:ng * 512], func=AF.Exp),
                 reads=[spk], writes=[ptk])
        else:
            for j in range(ng):
                P.op("act", lambda e, sp=sp, pt=pt, j=j: e.activation(out=pt[:, j * 512:j * 512 + N], in_=sp[:, j * 512:j * 512 + N], func=AF.Exp),
                     reads=[spk], writes=[ptk])
        for j, t in enumerate(grp):
            mm(P, O, t["v"], pt[:, j * 512:j * 512 + N], (i + j) == 0, (i + j) == nt - 1, [ptk] + t["vk"], [okey])
        i += 2


def phase_fox(P, C, S, PFb, PFf, VT, fb, OT):
    NB = S // 128
    NQC = S // 512
    with contextlib.ExitStack() as st:
        QF = P.sb("QF", [68, 4, S], BF16, st)
        KF = P.sb("KF", [68, 4, S], BF16, st)
        VA = P.sb("VA", [128, NB, 4, 128], BF16, st)
        P.op("pool", lambda e: e.memset(QF[64:68, :, :], 1.0), writes=["QF"])
        P.op("pool", lambda e: e.memset(KF[64:68, :, :], 1.0), writes=["KF"])
        for h in range(4):
            g, r0 = h // 2, (h % 2) * 64
            P.dma("sp", lambda e, h=h, g=g, r0=r0: e.dma_start(out=QF[0:64, h, :], in_=PFb[PFB_ROW["fq%d" % g] + r0:PFB_ROW["fq%d" % g] + r0 + 64, :]),
                  reads=["PFb"], writes=["QF"], pool="l", npool=6)
            P.dma("sp", lambda e, h=h, g=g, r0=r0: e.dma_start(out=KF[0:64, h, :], in_=PFb[PFB_ROW["fk%d" % g] + r0:PFB_ROW["fk%d" % g] + r0 + 64, :]),
                  reads=["PFb"], writes=["KF"], pool="l", npool=6)
        P.dma("pool", lambda e: e.dma_start(out=VA[:], in_=VT[:, 0:4, :].rearrange("(b p) h c -> p b h c", p=128)),
              reads=["VT"], writes=["VA"], pool="l2", npool=6)
        with contextlib.ExitStack() as s1:
            f4 = P.sb("f4", [4, S], F32, s1)
            t1 = P.sb("t1", [4, S], F32, s1)
            t2 = P.sb("t2", [4, S], F32, s1)
            cc = P.sb("cc", [4, S], F32, s1)
            hb = [P.sb("hb%d" % i, [4, S], BF16, s1) for i in range(4)]
            fbc = P.sb("fbc", [4, 1], F32, s1)
            one4 = P.sb("one4", [4, 512], F32, s1)
            P.op("pool", lambda e: e.memset(one4[:], 1.0), writes=["one4"])
            P.dma("sp", lambda e: e.dma_start(out=f4[:], in_=PFf[PFF_ROW["small"]:PFF_ROW["small"] + 4, :]), reads=["PFf"], writes=["f4"], pool="l", npool=6)
            P.dma("sp", lambda e: e.dma_start(out=fbc[:], in_=fb), writes=["fbc"], pool="l", npool=6)
            P.op("dve", lambda e: e.tensor_scalar(out=f4[:], in0=f4[:], scalar1=fbc[:, 0:1], scalar2=None, op0=ALU.add), reads=["f4", "fbc"], writes=["f4"])
            P.op("act", lambda e: e.activation(out=t1[:], in_=f4[:], func=AF.Abs), reads=["f4"], writes=["t1"])
            P.op("act", lambda e: e.activation(out=t1[:], in_=t1[:], func=AF.Exp, scale=-1.0), reads=["t1"], writes=["t1"])
            P.op("act", lambda e: e.activation(out=t1[:], in_=t1[:], func=AF.Ln, bias=C.onec[0:4, 0:1], scale=1.0), reads=["t1", "onec"], writes=["t1"])
            P.op("dve", lambda e: e.tensor_single_scalar(out=t2[:], in_=f4[:], scalar=0.0, op=ALU.min), reads=["f4"], writes=["t2"])
            P.op("dve", lambda e: e.tensor_tensor(out=t2[:], in0=t2[:], in1=t1[:], op=ALU.subtract), reads=["t1", "t2"], writes=["t2"])
            for qc in range(NQC):
                sl = slice(qc * 512, (qc + 1) * 512)
                init = 0.0 if qc == 0 else cc[:, qc * 512 - 1:qc * 512]
                P.op("dve", lambda e, sl=sl, init=init: e.tensor_tensor_scan(out=cc[:, sl], data0=one4[:], data1=t2[:, sl], initial=init,
                                                                               op0=ALU.mult, op1=ALU.add), reads=["t2", "one4", "cc"], writes=["cc"])
            P.op("dve", lambda e: e.tensor_single_scalar(out=t1[:], in_=cc[:], scalar=-1.0, op=ALU.mult), reads=["cc"], writes=["t1"])
            P.op("dve", lambda e: e.tensor_copy(out=hb[0][:], in_=t1[:]), reads=["t1"], writes=["hb0"])
            P.op("dve", lambda e: e.tensor_tensor(out=t1[:], in0=t1[:], in1=hb[0][:], op=ALU.subtract), reads=["t1", "hb0"], writes=["t1"])
            P.op("dve", lambda e: e.tensor_copy(out=hb[1][:], in_=t1[:]), reads=["t1"], writes=["hb1"])
            P.op("dve", lambda e: e.tensor_tensor(out=t1[:], in0=t1[:], in1=hb[1][:], op=ALU.subtract), reads=["t1", "hb1"], writes=["t1"])
            P.op("dve", lambda e: e.tensor_copy(out=hb[2][:], in_=t1[:]), reads=["t1"], writes=["hb2"])
            P.op("dve", lambda e: e.tensor_single_scalar(out=hb[3][:], in_=hb[0][:], scalar=-1.0, op=ALU.mult), reads=["hb0"], writes=["hb3"])
            for h in range(4):
                for lv in range(3):
                    P.dma("sp", lambda e, h=h, lv=lv: e.dma_start(out=KF[64 + lv:65 + lv, h, :], in_=hb[lv][h:h + 1, :]),
                          reads=["hb%d" % lv], writes=["KF"], pool="l", npool=6)
                P.dma("sp", lambda e, h=h: e.dma_start(out=QF[67:68, h, :], in_=hb[3][h:h + 1, :]), reads=["hb3"], writes=["QF"], pool="l", npool=6)
            P.barrier()
        with contextlib.ExitStack() as s2:
            GT = [P.sb("GT%d" % i, [64, S], BF16, s2) for i in range(2)]
            OS = [P.sb("OS%d" % i, [64, S], BF16, s2) for i in range(2)]
            rz = P.sb("rz", [64, 512], F32, s2)
            tmp = P.sb("tmp", [64, 512], F32, s2)
            for h in range(4):
                g, r0 = h // 2, (h % 2) * 64
                gt, os_ = GT[h % 2], OS[h % 2]
                gk, ok_ = "GT%d" % (h % 2), "OS%d" % (h % 2)
                P.dma("pool", lambda e, gt=gt, g=g, r0=r0: e.dma_start(out=gt[:], in_=PFb[PFB_ROW["fg%d" % g] + r0:PFB_ROW["fg%d" % g] + r0 + 64, :]),
                      reads=["PFb"], writes=[gk], pool="l2", npool=6)
                for qc in range(NQC):
                    qs = slice(qc * 512, (qc + 1) * 512)
                    tiles = []
                    for kb in range(4 * qc + 4):
                        t = dict(lhsT=KF[:, h, kb * 128:(kb + 1) * 128], rhs=QF[:, h, qs], rk=["KF", "QF"],
                                 v=VA[:, kb, h, :], vk=["VA"], extra=None)
                        if kb >= 4 * qc:
                            off = qc * 512 - kb * 128
                            t["extra"] = (C.Jb[:], C.WRc[:, off + 384:off + 384 + 512], ["Jb", "WRc"])
                        tiles.append(t)
                    ob = C.obank[C.obi % 2]
                    obk = "obank%d" % (C.obi % 2)
                    C.obi += 1
                    attn_tiles(P, C, tiles, ob[:], obk, 512)
                    P.op("dve", lambda e, ob=ob: e.reciprocal(out=rz[:], in_=ob[64:128, :]), reads=[obk], writes=["rz"])
                    P.op("dve", lambda e, ob=ob: e.tensor_tensor(out=tmp[:], in0=ob[0:64, :], in1=rz[:], op=ALU.mult), reads=[obk, "rz"], writes=["tmp"])
                    P.op("pool", lambda e, os_=os_, gt=gt, qs=qs: e.tensor_tensor(out=os_[:, qs], in0=tmp[:], in1=gt[:, qs], op=ALU.mult),
                         reads=["tmp", gk], writes=[ok_])
                P.dma("sp", lambda e, os_=os_, h=h: e.dma_start(out=OT[h * 64:(h + 1) * 64, :], in_=os_[:]), reads=[ok_], writes=["OT"], pool="o", npool=4)
    P.barrier()


def phase_mem(P, C, S, PFb, memT, wkv, mg_, OT):
    NQC = S // 512
    with contextlib.ExitStack() as st:
        mx = P.sb("mx", [128, 8, 256], F32, st)
        msq = P.sb("msq", [128, 8, 256], BF16, st)
        mh = P.sb("mh", [128, 8, 256], BF16, st)
        mrt = P.sb("mrt", [128, 256], F32, st)
        mgc = P.sb("mgc", [128, 8], F32, st)
        wks = P.sb("wks", [128, 8, 512], F32, st)
        wkb = P.sb("wkb", [128, 8, 512], BF16, st)
        KM = P.sb("KM", [128, 2, 256], BF16, st)
        VM = P.sb("VM", [128, 2, 2, 128], BF16, st)
        QM = P.sb("QM", [128, 2, S], BF16, st)
        GM = P.sb("GM", [128, 2, S], BF16, st)
        OM = P.sb("OM", [128, 2, S], BF16, st)
        rz = P.sb("rzm", [128, 512], F32, st)
        tmp = P.sb("tmpm", [128, 512], F32, st)
        P.dma("sp", lambda e: e.dma_start(out=mx[:], in_=memT.rearrange("(k p) t -> p k t", p=128)), writes=["mx"], pool="l", npool=6)
        P.dma("sp", lambda e: e.dma_start(out=mgc[:], in_=mg_), writes=["mgc"], pool="l", npool=6)
        P.dma("pool", lambda e: e.dma_start(out=wks[:], in_=wkv.rearrange("(k p) c -> p k c", p=128)), writes=["wks"], pool="l2", npool=6)
        for g in range(2):
            P.dma("sp", lambda e, g=g: e.dma_start(out=QM[:, g, :], in_=PFb[PFB_ROW["mq%d" % g]:PFB_ROW["mq%d" % g] + 128, :]),
                  reads=["PFb"], writes=["QM"], pool="l", npool=6)
            P.dma("pool", lambda e, g=g: e.dma_start(out=GM[:, g, :], in_=PFb[PFB_ROW["mg%d" % g]:PFB_ROW["mg%d" % g] + 128, :]),
                  reads=["PFb"], writes=["GM"], pool="l2", npool=6)
        P.op("pool", lambda e: e.tensor_copy(out=wkb[:], in_=wks[:]), reads=["wks"], writes=["wkb"])
        P.op("act", lambda e: e.activation(out=msq[:], in_=mx[:], func=AF.Square), reads=["mx"], writes=["msq"])
        ps = C.psb[0]
        for k in range(8):
            mm(P, ps[:, 0:256], C.onesb[:, 0:128], msq[:, k, :], k == 0, k == 7, ["msq", "onesb"], ["psb0"])
        P.op("act", lambda e, ps=ps: e.activation(out=mrt[:], in_=ps[:, 0:256], func=AF.Sqrt, bias=C.epsc[:, 0:1], scale=1.0 / D_MODEL),
             reads=["psb0", "epsc"], writes=["mrt"])
        P.op("dve", lambda e: e.reciprocal(out=mrt[:], in_=mrt[:]), reads=["mrt"], writes=["mrt"])
        for k in range(8):
            P.op("dve", lambda e, k=k: e.scalar_tensor_tensor(out=mh[:, k, :], in0=mx[:, k, :], scalar=mgc[:, k:k + 1], in1=mrt[:],
                                                              op0=ALU.mult, op1=ALU.mult), reads=["mx", "mrt", "mgc"], writes=["mh"])
        for h in range(2):
            ps = C.psb[h % 2]
            pk = "psb%d" % (h % 2)
            for k in range(8):
                mm(P, ps[:, 0:256], wkb[:, k, h * 128:(h + 1) * 128], mh[:, k, :], k == 0, k == 7, ["wkb", "mh"], [pk])
            P.op("act", lambda e, h=h, ps=ps: e.activation(out=KM[:, h, :], in_=ps[:, 0:256], func=AF.Copy), reads=[pk], writes=["KM"])
        for mc in range(2):
            ps = C.psb[mc % 2]
            pk = "psb%d" % (mc % 2)
            for k in range(8):
                mm(P, ps[:, 0:256], mh[:, k, mc * 128:(mc + 1) * 128], wkb[:, k, 256:512], k == 0, k == 7, ["wkb", "mh"], [pk])
            P.op("act", lambda e, mc=mc, ps=ps: e.activation(out=VM[:, mc, :, :], in_=ps[:, 0:256].rearrange("p (h d) -> p h d", d=128), func=AF.Copy),
                 reads=[pk], writes=["VM"])
        for h in range(2):
            for qc in range(NQC):
                qs = slice(qc * 512, (qc + 1) * 512)
                sp = C.spair[C.spi % 2]
                spk = "spair%d" % (C.spi % 2)
                pt = C.ptile[C.spi % 2]
                ptk = "ptile%d" % (C.spi % 2)
                C.spi += 1
                for mc in range(2):
                    mm(P, sp[:, mc * 512:(mc + 1) * 512], KM[:, h, mc * 128:(mc + 1) * 128], QM[:, h, qs], True, True, ["KM", "QM"], [spk])
                P.op("act", lambda e, sp=sp, pt=pt: e.activation(out=pt[:], in_=sp[:], func=AF.Exp), reads=[spk], writes=[ptk])
                ob = C.obank[0]
                zb = C.obank[1]
                for mc in range(2):
                    mm(P, ob[:], VM[:, mc, h, :], pt[:, mc * 512:(mc + 1) * 512], mc == 0, mc == 1, [ptk, "VM"], ["obank0"])
                for mc in range(2):
                    mm(P, zb[:], C.onesb[:, 0:128], pt[:, mc * 512:(mc + 1) * 512], mc == 0, mc == 1, [ptk, "onesb"], ["obank1"])
                P.op("dve", lambda e: e.reciprocal(out=rz[:], in_=zb[:]), reads=["obank1"], writes=["rzm"])
                P.op("dve", lambda e: e.tensor_tensor(out=tmp[:], in0=ob[:], in1=rz[:], op=ALU.mult), reads=["obank0", "rzm"], writes=["tmpm"])
                P.op("pool", lambda e, h=h, qs=qs: e.tensor_tensor(out=OM[:, h, qs], in0=tmp[:], in1=GM[:, h, qs], op=ALU.mult),
                     reads=["tmpm", "GM"], writes=["OM"])
        P.dma("sp", lambda e: e.dma_start(out=OT[768:1024, :].rearrange("(h d) t -> d h t", d=128), in_=OM[:]), reads=["OM"], writes=["OT"], pool="o", npool=4)
    P.barrier()


def make_consts():
    J = np.zeros((128, 128), np.float32)
    J[np.arange(128), 127 - np.arange(128)] = 1.0
    k = np.arange(128)[:, None]
    j = np.arange(896)[None, :]
    W = np.where(j - 384 - k >= 0, 0.0, NEG).astype(np.float32)
    WRc = W[::-1, :].copy()
    return {"cJ": J.astype(ml_dtypes.bfloat16), "cWRc": WRc.astype(ml_dtypes.bfloat16)}


def setup_ctx(P, nc, cJ, cWRc):
    C = Ctx()
    C.onesb = P.sb("onesb", [128, 128], BF16)
    C.epsc = P.sb("epsc", [128, 1], F32)
    C.onec = P.sb("onec", [128, 1], F32)
    C.Jb = P.sb("Jb", [128, 128], BF16)
    C.WRc = P.sb("WRc", [128, 896], BF16)
    C.psb = [P.ps("psb%d" % i, [128, 512]) for i in range(2)]
    C.spair = [P.ps("spair%d" % i, [128, 1024]) for i in range(2)]
    C.obank = [P.ps("obank%d" % i, [128, 512]) for i in range(2)]
    C.ptile = [P.sb("ptile%d" % i, [128, 1024], BF16) for i in range(2)]
    C.spi = 0
    C.obi = 0
    P.op("pool", lambda e: e.memset(C.onesb[:], 1.0), writes=["onesb"])
    P.op("pool", lambda e: e.memset(C.epsc[:], EPS), writes=["epsc"])
    P.op("pool", lambda e: e.memset(C.onec[:], 1.0), writes=["onec"])
    P.dma("sp", lambda e: e.dma_start(out=C.Jb[:], in_=cJ), writes=["Jb"])
    P.dma("sp", lambda e: e.dma_start(out=C.WRc[:], in_=cWRc), writes=["WRc"])
    return C


def bc(ap, shape):
    return ap.broadcast_to(list(shape))


def phase_ssd(P, C, S, PFb, PFf, cw, cb, dtb, alog, dsk, ngs, OT):
    NB = S // 128
    NQC = S // 512
    with contextlib.ExitStack() as st:
        XC = P.sb("XC", [128, 4, S], BF16, st)
        XT = P.sb("XT", [128, NB, 384], BF16, st)
        SZ = P.sb("SZ", [128, 2, S], BF16, st)
        cwt = P.sb("cwt", [128, 4, 4], F32, st)
        cbt = P.sb("cbt", [128, 4], F32, st)
        dtbt = P.sb("dtbt", [4, 1], F32, st)
        alt = P.sb("alt", [4, 1], F32, st)
        dskt = P.sb("dskt", [128, 256], F32, st)
        ngt = P.sb("ngt", [128, 2], F32, st)
        dtT = P.sb("dtT", [128, NB, 4], F32, st)
        aT = P.sb("aT", [128, NB, 4], F32, st)
        el = P.sb("el", [128, NB, 4], F32, st)
        dec = P.sb("dec", [128, NB, 4], F32, st)
        dtw = P.sb("dtw", [128, NB, 4], F32, st)
        acs = P.sb("acs", [128, NB, 4], F32, st)
        for nm, t, src in (("cwt", cwt, cw), ("cbt", cbt, cb), ("dtbt", dtbt, dtb), ("alt", alt, alog), ("dskt", dskt, dsk), ("ngt", ngt, ngs)):
            P.dma("sp", lambda e, t=t, src=src: e.dma_start(out=t[:], in_=src), writes=[nm], pool="l", npool=6)
        for g in range(2):
            P.dma("pool", lambda e, g=g: e.dma_start(out=SZ[:, g, :], in_=PFb[PFB_ROW["sz%d" % g]:PFB_ROW["sz%d" % g] + 128, :]),
                  reads=["PFb"], writes=["SZ"], pool="l2", npool=6)
        with contextlib.ExitStack() as sa:
            XP = [P.sb("XP%d" % i, [128, S + 3], F32, sa) for i in range(2)]
            acc = P.sb("acc", [128, S], F32, sa)
            dr = P.sb("dr", [4, S], F32, sa)
            d1 = P.sb("d1", [4, S], F32, sa)
            d2 = P.sb("d2", [4, S], F32, sa)
            for i in range(2):
                P.op("pool", lambda e, i=i: e.memset(XP[i][:, 0:3], 0.0), writes=["XP%d" % i])
            names = ["sx0", "sx1", "sB", "sC"]
            for g in range(4):
                xp = XP[g % 2]
                xk = "XP%d" % (g % 2)
                r0 = PFF_ROW[names[g]]
                P.dma("sp", lambda e, xp=xp, r0=r0: e.dma_start(out=xp[:, 3:3 + S], in_=PFf[r0:r0 + 128, :]), reads=["PFf"], writes=[xk], pool="l", npool=6)
                eng = "dve"
                P.op(eng, lambda e, xp=xp, g=g: e.tensor_scalar(out=acc[:], in0=xp[:, 0:S], scalar1=cwt[:, g, 0:1], scalar2=None, op0=ALU.mult),
                     reads=[xk, "cwt"], writes=["acc"])
                for k in range(1, 4):
                    P.op(eng, lambda e, xp=xp, g=g, k=k: e.scalar_tensor_tensor(out=acc[:], in0=xp[:, k:k + S], scalar=cwt[:, g, k:k + 1], in1=acc[:],
                                                                                op0=ALU.mult, op1=ALU.add), reads=[xk, "cwt", "acc"], writes=["acc"])
                P.op("act", lambda e, g=g: e.activation(out=XC[:, g, :], in_=acc[:], func=AF.Silu, bias=cbt[:, g:g + 1], scale=1.0),
                     reads=["acc", "cbt"], writes=["XC"])
            P.dma("sp", lambda e: e.dma_start(out=dr[:], in_=PFf[PFF_ROW["small"] + 4:PFF_ROW["small"] + 8, :]), reads=["PFf"], writes=["dr"], pool="l", npool=6)
            P.op("dve", lambda e: e.tensor_scalar(out=dr[:], in0=dr[:], scalar1=dtbt[:, 0:1], scalar2=None, op0=ALU.add), reads=["dr", "dtbt"], writes=["dr"])
            P.op("act", lambda e: e.activation(out=d1[:], in_=dr[:], func=AF.Abs), reads=["dr"], writes=["d1"])
            P.op("act", lambda e: e.activation(out=d1[:], in_=d1[:], func=AF.Exp, scale=-1.0), reads=["d1"], writes=["d1"])
            P.op("act", lambda e: e.activation(out=d1[:], in_=d1[:], func=AF.Ln, bias=C.onec[0:4, 0:1], scale=1.0), reads=["d1", "onec"], writes=["d1"])
            P.op("dve", lambda e: e.tensor_single_scalar(out=d2[:], in_=dr[:], scalar=0.0, op=ALU.max), reads=["dr"], writes=["d2"])
            P.op("dve", lambda e: e.tensor_tensor(out=d1[:], in0=d1[:], in1=d2[:], op=ALU.add), reads=["d1", "d2"], writes=["d1"])
            P.op("act", lambda e: e.activation(out=alt[:], in_=alt[:], func=AF.Exp), reads=["alt"], writes=["alt"])
            P.op("dve", lambda e: e.tensor_scalar(out=d2[:], in0=d1[:], scalar1=alt[:, 0:1], scalar2=-1.0, op0=ALU.mult, op1=ALU.mult),
                 reads=["d1", "alt"], writes=["d2"])
            for c in range(NB):
                P.op("pe", lambda e, c=c: e.transpose(out=C.psb[0][:, c * 4:(c + 1) * 4], in_=d1[0:4, c * 128:(c + 1) * 128], identity=C.identf[0:4, 0:4]),
                     reads=["d1", "identf"], writes=["psb0"])
                P.op("pe", lambda e, c=c: e.transpose(out=C.psb[1][:, c * 4:(c + 1) * 4], in_=d2[0:4, c * 128:(c + 1) * 128], identity=C.identf[0:4, 0:4]),
                     reads=["d2", "identf"], writes=["psb1"])
            P.op("dve", lambda e: e.tensor_copy(out=dtT[:].rearrange("p c h -> p (c h)"), in_=C.psb[0][:, 0:NB * 4]), reads=["psb0"], writes=["dtT"])
            P.op("dve", lambda e: e.tensor_copy(out=aT[:].rearrange("p c h -> p (c h)"), in_=C.psb[1][:, 0:NB * 4]), reads=["psb1"], writes=["aT"])
            aflat = aT[:].rearrange("p c h -> p (c h)")
            mm(P, C.psb[0][:, 0:NB * 4], C.Umat[:], aflat, True, True, ["Umat", "aT"], ["psb0"])
            mm(P, C.psb[1][:, 0:NB * 4], C.onesf[:], aflat, True, True, ["onesf", "aT"], ["psb1"])
            fl = lambda t: t[:].rearrange("p c h -> p (c h)")
            P.op("act", lambda e: e.activation(out=fl(el), in_=C.psb[0][:, 0:NB * 4], func=AF.Exp), reads=["psb0"], writes=["el"])
            P.op("act", lambda e: e.activation(out=fl(dec), in_=C.psb[1][:, 0:NB * 4], func=AF.Exp), reads=["psb1"], writes=["dec"])
            P.op("act", lambda e: e.activation(out=fl(acs), in_=C.psb[0][:, 0:NB * 4], func=AF.Copy), reads=["psb0"], writes=["acs"])
            P.op("dve", lambda e: e.tensor_tensor(out=fl(dtw), in0=C.psb[1][:, 0:NB * 4], in1=fl(acs), op=ALU.subtract), reads=["psb1", "acs"], writes=["dtw"])
            P.op("act", lambda e: e.activation(out=fl(dtw), in_=fl(dtw), func=AF.Exp), reads=["dtw"], writes=["dtw"])
            P.op("dve", lambda e: e.tensor_tensor(out=fl(dtw), in0=fl(dtw), in1=fl(dtT), op=ALU.mult), reads=["dtw", "dtT"], writes=["dtw"])
            for c in range(NB):
                pv = C.psb[c % 2][:].bitcast(BF16)
                pk = "psb%d" % (c % 2)
                for j, g in enumerate((0, 1, 2)):
                    P.op("pe", lambda e, pv=pv, j=j, g=g, c=c: e.transpose(out=pv[:, j * 128:(j + 1) * 128], in_=XC[:, g, c * 128:(c + 1) * 128], identity=C.identb[:]),
                         reads=["XC", "identb"], writes=[pk])
                if c % 2 == 0:
                    P.op("act", lambda e, pv=pv, c=c: e.activation(out=XT[:, c, :], in_=pv[:, 0:384], func=AF.Copy), reads=[pk], writes=["XT"])
                else:
                    P.op("dve", lambda e, pv=pv, c=c: e.tensor_copy(out=XT[:, c, :], in_=pv[:, 0:384]), reads=[pk], writes=["XT"])
        P.barrier()
        with contextlib.ExitStack() as sb_:
            Y = P.sb("Y", [128, NB, 256], F32, sb_)
            R1 = P.sb("R1", [128, 4, 128], F32, sb_)
            R2 = P.sb("R2", [128, 4, 128], F32, sb_)
            ED = P.sb("ED", [128, 4, 128], F32, sb_)
            MT = P.sb("MT", [128, 4, 128], BF16, sb_)
            xdt = P.sb("xdt", [128, 4, 64], BF16, sb_)
            xdw = P.sb("xdw", [128, 4, 64], BF16, sb_)
            st32 = P.sb("st32", [128, 4, 64], F32, sb_)
            stb_ = P.sb("stbf", [128, 4, 64], BF16, sb_)
            tq = P.sb("tq", [128, 4, 64], F32, sb_)
            t2 = P.sb("t2s", [128, 256], F32, sb_)
            P.op("pool", lambda e: e.memset(st32[:], 0.0), writes=["st32"])
            P.op("pool", lambda e: e.memset(stb_[:], 0.0), writes=["stbf"])
            for c in range(NB):
                cs_ = slice(c * 128, (c + 1) * 128)
                a_bc = bc(aT[:, c, :].unsqueeze(2), [128, 4, 128])
                P.op("dve", lambda e, a_bc=a_bc: e.tensor_tensor(out=R1[:], in0=bc(C.Umat[:].unsqueeze(1), [128, 4, 128]), in1=a_bc, op=ALU.mult),
                     reads=["Umat", "aT"], writes=["R1"])
                P.op("pool", lambda e, a_bc=a_bc: e.tensor_tensor(out=R2[:], in0=bc(C.SHm[:].unsqueeze(1), [128, 4, 128]), in1=a_bc, op=ALU.subtract),
                     reads=["SHm", "aT"], writes=["R2"])
                sp = C.spair[c % 2]
                spk = "spair%d" % (c % 2)
                mm(P, sp[:, 0:512], C.onesf[:], R1[:].rearrange("p h l -> p (h l)"), True, False, ["onesf", "R1"], [spk])
                mm(P, sp[:, 0:512], C.Umat[:], R2[:].rearrange("p h l -> p (h l)"), False, True, ["Umat", "R2"], [spk])
                P.op("act", lambda e, sp=sp: e.activation(out=ED[:].rearrange("p h l -> p (h l)"), in_=sp[:, 0:512], func=AF.Exp), reads=[spk], writes=["ED"])
                pg = C.psb[c % 2]
                pgk = "psb%d" % (c % 2)
                mm(P, pg[:, 0:128], XC[:, 2, cs_], XC[:, 3, cs_], True, True, ["XC"], [pgk])
                P.op("dve", lambda e, pg=pg: e.tensor_tensor(out=MT[:], in0=ED[:], in1=bc(pg[:, 0:128].unsqueeze(1), [128, 4, 128]), op=ALU.mult),
                     reads=["ED", pgk], writes=["MT"])
                xv = XT[:, c, 0:256].rearrange("p (h d) -> p h d", d=64)
                P.op("pool", lambda e, xv=xv, c=c: e.tensor_tensor(out=xdt[:], in0=xv, in1=bc(dtT[:, c, :].unsqueeze(2), [128, 4, 64]), op=ALU.mult),
                     reads=["XT", "dtT"], writes=["xdt"])
                P.op("pool", lambda e, xv=xv, c=c: e.tensor_tensor(out=xdw[:], in0=xv, in1=bc(dtw[:, c, :].unsqueeze(2), [128, 4, 64]), op=ALU.mult),
                     reads=["XT", "dtw"], writes=["xdw"])
                py = C.obank[0]
                po = C.obank[1]
                for h in range(4):
                    mm(P, py[:, h * 64:(h + 1) * 64], MT[:, h, :], xdt[:, h, :], True, True, ["MT", "xdt"], ["obank0"])
                mm(P, po[:, 0:256], XC[:, 3, cs_], stb_[:].rearrange("p h d -> p (h d)"), True, True, ["XC", "stbf"], ["obank1"])
                pc = C.psb[(c + 1) % 2]
                pck = "psb%d" % ((c + 1) % 2)
                mm(P, pc[:, 256:512], XT[:, c, 256:384], xdw[:].rearrange("p h d -> p (h d)"), True, True, ["XT", "xdw"], [pck])
                P.op("dve", lambda e, c=c: e.tensor_tensor(out=tq[:], in0=po[:, 0:256].rearrange("p (h d) -> p h d", d=64),
                                                          in1=bc(el[:, c, :].unsqueeze(2), [128, 4, 64]), op=ALU.mult), reads=["obank1", "el"], writes=["tq"])
                P.op("pool", lambda e, c=c: e.tensor_tensor(out=t2[:], in0=XT[:, c, 0:256], in1=dskt[:], op=ALU.mult), reads=["XT", "dskt"], writes=["t2s"])
                P.op("dve", lambda e, c=c: e.tensor_tensor(out=Y[:, c, :], in0=py[:, 0:256], in1=tq[:].rearrange("p h d -> p (h d)"), op=ALU.add),
                     reads=["obank0", "tq"], writes=["Y"])
                P.op("pool", lambda e, c=c: e.tensor_tensor(out=Y[:, c, :], in0=Y[:, c, :], in1=t2[:], op=ALU.add), reads=["Y", "t2s"], writes=["Y"])
                P.op("pool", lambda e, c=c: e.tensor_tensor(out=st32[:], in0=st32[:], in1=bc(dec[:, c, :].unsqueeze(2), [128, 4, 64]), op=ALU.mult),
                     reads=["st32", "dec", "stbf"], writes=["st32"])
                P.op("dve", lambda e, pc=pc: e.tensor_tensor(out=st32[:], in0=pc[:, 256:512].rearrange("p (h d) -> p h d", d=64), in1=st32[:], op=ALU.add),
                     reads=[pck, "st32"], writes=["st32"])
                P.op("act", lambda e: e.activation(out=stb_[:], in_=st32[:], func=AF.Copy), reads=["st32"], writes=["stbf"])
            with contextlib.ExitStack() as sc:
                YG = P.sb("YG", [128, 2, S], F32, sc)
                sqy = P.sb("sqy", [128, 2, 512], BF16, sc)
                rty = P.sb("rty", [128, 512], F32, sc)
                OSs = P.sb("OSs", [128, 2, S], BF16, sc)
                for c in range(NB):
                    pt_ = C.spair[c % 2]
                    ptk = "spair%d" % (c % 2)
                    for g in range(2):
                        P.op("pe", lambda e, pt_=pt_, g=g, c=c: e.transpose(out=pt_[:, g * 128:(g + 1) * 128], in_=Y[:, c, g * 128:(g + 1) * 128], identity=C.identf[:]),
                             reads=["Y", "identf"], writes=[ptk])
                    P.op("dve", lambda e, pt_=pt_, c=c: e.tensor_tensor(out=YG[:, :, c * 128:(c + 1) * 128], in0=pt_[:, 0:256].rearrange("p (g t) -> p g t", t=128),
                                                                     in1=SZ[:, :, c * 128:(c + 1) * 128], op=ALU.mult), reads=[ptk, "SZ"], writes=["YG"])
                for qc in range(NQC):
                    qs = slice(qc * 512, (qc + 1) * 512)
                    P.op("act", lambda e, qs=qs: e.activation(out=sqy[:], in_=YG[:, :, qs], func=AF.Square), reads=["YG"], writes=["sqy"])
                    ps = C.psb[qc % 2]
                    pk = "psb%d" % (qc % 2)
                    for g in range(2):
                        mm(P, ps[:], C.onesb[:], sqy[:, g, :], g == 0, g == 1, ["sqy", "onesb"], [pk])
                    P.op("act", lambda e, ps=ps: e.activation(out=rty[:], in_=ps[:], func=AF.Sqrt, bias=C.epsc[:, 0:1], scale=1.0 / 256), reads=[pk, "epsc"], writes=["rty"])
                    P.op("dve", lambda e: e.reciprocal(out=rty[:], in_=rty[:]), reads=["rty"], writes=["rty"])
                    for g in range(2):
                        P.op("dve", lambda e, g=g, qs=qs: e.scalar_tensor_tensor(out=OSs[:, g, qs], in0=YG[:, g, qs], scalar=ngt[:, g:g + 1], in1=rty[:],
                                                                                              op0=ALU.mult, op1=ALU.mult), reads=["YG", "ngt", "rty"], writes=["OSs"])
                P.dma("sp", lambda e: e.dma_start(out=OT[256:512, :].rearrange("(g p) t -> p g t", p=128), in_=OSs[:]), reads=["OSs"], writes=["OT"], pool="o", npool=4)
                P.barrier()
    P.barrier()


TINY = 1e-30
GC_OFF = 4111


def nsa_host(S, table4, pe, w1, w2):
    nsel = S // 64
    NB = S // 128
    n_cmp = S // 16 - 1
    text = np.concatenate([table4, np.full((1, 4), NEG, np.float32)], 0)
    LC = S + 4080
    dC = np.arange(LC) - GC_OFF
    idxC = np.where(dC < 0, 32, t5_bucket(dC))
    dS = np.arange(383) - 127
    idxS = np.where(dS < 0, 32, t5_bucket(dS))
    dW = np.arange(767) - 127
    idxW = np.where((dW < 0) | (dW >= 512), 32, t5_bucket(dW))
    GALL = np.ascontiguousarray(np.concatenate([text[idxC], text[idxS], text[idxW]], 0).T.astype(np.float32))
    t31 = np.ascontiguousarray(table4[31].reshape(4, 1))
    E = np.zeros((64, S), np.float32)
    kk = np.arange(S)
    E[kk // 64, kk] = 1.0
    NCC = (n_cmp + 127) // 128
    cidx = np.arange(NCC * 128)
    cstart = cidx * 16
    ss = np.arange(nsel) * 64
    ov = ((cstart[:, None] < ss[None, :] + 64) & (cstart[:, None] + 32 > ss[None, :]) & (cidx[:, None] < n_cmp)).astype(np.float32)
    OVL = np.concatenate([ov, np.ones((NCC * 128, 1), np.float32)], 1).reshape(NCC, 128, nsel + 1).transpose(1, 0, 2)
    qpos = np.arange(S)
    cur = qpos // 64
    jj = np.arange(nsel)[None, :]
    forced = (jj == 0) | (jj == cur[:, None]) | (jj == cur[:, None] - 1)
    valid = jj <= cur[:, None]
    FA = (valid & ~forced).astype(np.float32)
    FB = np.where(valid, np.where(forced, 1e9, 0.0), -1e9).astype(np.float32)
    FA = FA.reshape(NB, 128, nsel).transpose(1, 0, 2)
    FB = FB.reshape(NB, 128, nsel).transpose(1, 0, 2)
    pet = np.concatenate([pe[0].T, pe[1].T], 0)
    w1t = np.concatenate([w1[0].reshape(32, 64, 128).transpose(1, 0, 2), w1[1].reshape(32, 64, 128).transpose(1, 0, 2)], 0)
    w2t = np.concatenate([w2[0], w2[1]], 1)
    return {"nGALL": GALL, "nt31": t31, "nE": E.astype(ml_dtypes.bfloat16), "nOVL": np.ascontiguousarray(OVL).astype(ml_dtypes.bfloat16),
            "nFA": np.ascontiguousarray(FA), "nFB": np.ascontiguousarray(FB), "npe": np.ascontiguousarray(pet.astype(np.float32)),
            "nw1": np.ascontiguousarray(w1t.astype(np.float32)), "nw2": np.ascontiguousarray(w2t.astype(np.float32))}


def phase_nsa(P, C, S, PFb, PFf, VT, A, GALLd, GLd, OT):
    NB = S // 128
    nsel = S // 64
    n_cmp = S // 16 - 1
    NCC = (n_cmp + 127) // 128
    LC = S + 4080
    LALL = LC + 383 + 767
    with contextlib.ExitStack() as st:
        QN = P.sb("QN", [128, 4, S], BF16, st)
        KS = P.sb("KS", [128, S], BF16, st)
        KW = P.sb("KW", [64, S], BF16, st)
        VS = P.sb("VS", [128, NB, 128], BF16, st)
        VW = P.sb("VW", [128, NB, 128], BF16, st)
        KCM = P.sb("KCM", [64, NCC * 128], BF16, st)
        VCM = P.sb("VCM", [128, NCC, 128], BF16, st)
        OVL = P.sb("OVL", [128, NCC, nsel + 1], BF16, st)
        FA = P.sb("FA", [128, NB, nsel], F32, st)
        FB = P.sb("FB", [128, NB, nsel], F32, st)
        WRs = P.sb("WRs", [128, 4, 256], BF16, st)
        WRw = P.sb("WRw", [128, 4, 640], BF16, st)
        P.op("pool", lambda e: e.memset(QN[64:128, :, :], 0.0), writes=["QNm"])
        for h in range(4):
            g, r0 = h // 2, (h % 2) * 64
            P.dma("sp", lambda e, h=h, g=g, r0=r0: e.dma_start(out=QN[0:64, h, :], in_=PFb[PFB_ROW["nq%d" % g] + r0:PFB_ROW["nq%d" % g] + r0 + 64, :]),
                  reads=["PFb"], writes=["QNq"], pool="l", npool=6)
        r_kk = PFB_ROW["nkk"]
        P.dma("sp", lambda e: e.dma_start(out=KS[0:64, :], in_=PFb[r_kk:r_kk + 64, :]), reads=["PFb"], writes=["KS"], pool="l", npool=6)
        P.dma("sp", lambda e: e.dma_start(out=KS[64:128, :], in_=A["nE"]), writes=["KS"], pool="l", npool=6)
        P.dma("sp", lambda e: e.dma_start(out=KW[:], in_=PFb[r_kk + 64:r_kk + 128, :]), reads=["PFb"], writes=["KW"], pool="l", npool=6)
        P.dma("pool", lambda e: e.dma_start(out=VS[:], in_=VT[:, 4, :].rearrange("(b p) c -> p b c", p=128)), reads=["VT"], writes=["VS"], pool="l2", npool=6)
        P.dma("pool", lambda e: e.dma_start(out=VW[:], in_=VT[:, 5, :].rearrange("(b p) c -> p b c", p=128)), reads=["VT"], writes=["VW"], pool="l2", npool=6)
        P.dma("pool", lambda e: e.dma_start(out=OVL[:], in_=A["nOVL"]), writes=["OVL"], pool="l2", npool=6)
        P.dma("pool", lambda e: e.dma_start(out=FA[:], in_=A["nFA"]), writes=["FA"], pool="l2", npool=6)
        P.dma("pool", lambda e: e.dma_start(out=FB[:], in_=A["nFB"]), writes=["FB"], pool="l2", npool=6)
        with contextlib.ExitStack() as s1:
            gl = P.sb("gl", [12, S], F32, s1)
            gall = P.sb("gall", [4, LALL], F32, s1)
            t31 = P.sb("t31", [4, 1], F32, s1)
            wrf = P.sb("wrf", [128, 4, 640], F32, s1)
            P.dma("sp", lambda e: e.dma_start(out=gl[:], in_=PFf[PFF_ROW["small"] + 8:PFF_ROW["small"] + 20, :]), reads=["PFf"], writes=["gl"], pool="l", npool=6)
            P.op("act", lambda e: e.activation(out=gl[:], in_=gl[:], func=AF.Sigmoid), reads=["gl"], writes=["gl"])
            P.dma("sp", lambda e: e.dma_start(out=GLd, in_=gl[:]), reads=["gl"], writes=["GLd"], pool="o", npool=4)
            P.dma("sp", lambda e: e.dma_start(out=gall[:], in_=A["nGALL"]), writes=["gall"], pool="l", npool=6)
            P.dma("sp", lambda e: e.dma_start(out=t31[:], in_=A["nt31"]), writes=["t31"], pool="l", npool=6)
            P.op("dve", lambda e: e.tensor_scalar(out=gall[:], in0=gall[:], scalar1=t31[:, 0:1], scalar2=None, op0=ALU.subtract), reads=["gall", "t31"], writes=["gall"])
            P.dma("sp", lambda e: e.dma_start(out=GALLd, in_=gall[:]), reads=["gall"], writes=["GALLd"], pool="o", npool=4)
            hk = lambda off, n: bass.AP(tensor=GALLd.tensor, offset=off, ap=[[1, 128], [LALL, 4], [1, n]])
            P.dma("sp", lambda e: e.dma_start(out=wrf[:, :, 0:256], in_=hk(LC, 256)), reads=["GALLd"], writes=["wrf"], pool="l", npool=6)
            P.op("dve", lambda e: e.tensor_copy(out=WRs[:], in_=wrf[:, :, 0:256]), reads=["wrf"], writes=["WRs"])
            P.dma("sp", lambda e: e.dma_start(out=wrf[:], in_=hk(LC + 383, 640)), reads=["GALLd"], writes=["wrf"], pool="l", npool=6)
            P.op("dve", lambda e: e.tensor_copy(out=WRw[:], in_=wrf[:]), reads=["wrf"], writes=["WRw"])
            P.barrier()
            s1.close()
            s1b = contextlib.ExitStack()
            KCV = P.sb("KCV", [128, S], BF16, s1b)
            KA = P.sb("KA", [128, S], BF16, s1b)
            KB = P.sb("KB", [128, S], BF16, s1b)
            pet = P.sb("pet", [128, 32], F32, s1b)
            w1s = P.sb("w1s", [128, 32, 128], F32, s1b)
            w1b = P.sb("w1b", [128, 32, 128], BF16, s1b)
            w2s = P.sb("w2s", [128, 128], F32, s1b)
            w2b = P.sb("w2b", [128, 128], BF16, s1b)
            HS = P.sb("HS", [128, 2, NCC * 128], BF16, s1b)
            P.dma("sp", lambda e: e.dma_start(out=KCV[:], in_=PFb[PFB_ROW["ncv"]:PFB_ROW["ncv"] + 128, :]), reads=["PFb"], writes=["KCV"], pool="l", npool=6)
            P.dma("sp", lambda e: e.dma_start(out=pet[:], in_=A["npe"]), writes=["pet"], pool="l", npool=6)
            P.dma("pool", lambda e: e.dma_start(out=w1s[:], in_=A["nw1"]), writes=["w1s"], pool="l2", npool=6)
            P.dma("pool", lambda e: e.dma_start(out=w2s[:], in_=A["nw2"]), writes=["w2s"], pool="l2", npool=6)
            P.op("pool", lambda e: e.tensor_copy(out=w1b[:], in_=w1s[:]), reads=["w1s"], writes=["w1b"])
            P.op("pool", lambda e: e.tensor_copy(out=w2b[:], in_=w2s[:]), reads=["w2s"], writes=["w2b"])
            kv3 = KCV[:].rearrange("p (a b) -> p a b", b=16)
            P.op("dve", lambda e: e.tensor_tensor(out=KA[:].rearrange("p (a b) -> p a b", b=16), in0=kv3, in1=bc(pet[:, 0:16].unsqueeze(1), [128, S // 16, 16]), op=ALU.add),
                 reads=["KCV", "pet"], writes=["KA"])
            P.op("pool", lambda e: e.tensor_tensor(out=KB[:].rearrange("p (a b) -> p a b", b=16), in0=kv3, in1=bc(pet[:, 16:32].unsqueeze(1), [128, S // 16, 16]), op=ALU.add),
                 reads=["KCV", "pet"], writes=["KB"])
            P.op("pool", lambda e: e.memset(KCM[:], 0.0), writes=["KCM"])
            P.op("pool", lambda e: e.memset(VCM[:], 1.0), writes=["VCM"])
            P.op("pool", lambda e: e.memset(HS[:], 0.0), writes=["HS"])
            for kv in range(2):
                pr = slice(kv * 64, kv * 64 + 64)
                for cc in range(NCC):
                    c0 = cc * 128
                    ncc = min(128, n_cmp - c0)
                    ps = C.psb[(kv * NCC + cc) % 2]
                    pk = "psb%d" % ((kv * NCC + cc) % 2)
                    for l in range(32):
                        src = KA if l < 16 else KB
                        st0 = 16 * c0 + l
                        rhs = src[pr, st0:st0 + 16 * (ncc - 1) + 1:16]
                        mm(P, ps[:, 0:ncc], w1b[pr, l, :], rhs, l == 0, l == 31, ["w1b", "KA", "KB"], [pk])
                    P.op("act", lambda e, ps=ps, kv=kv, c0=c0, ncc=ncc: e.activation(out=HS[:, kv, c0:c0 + ncc], in_=ps[:, 0:ncc], func=AF.Silu), reads=[pk], writes=["HS"])
            for cc in range(NCC):
                c0 = cc * 128
                ncc = min(128, n_cmp - c0)
                ps = C.psb[cc % 2]
                pk = "psb%d" % (cc % 2)
                mm(P, ps[0:64, 0:ncc], w2b[:, 0:64], HS[:, 0, c0:c0 + ncc], True, True, ["w2b", "HS"], [pk])
                P.op("act", lambda e, ps=ps, c0=c0, ncc=ncc: e.activation(out=KCM[:, c0:c0 + ncc], in_=ps[0:64, 0:ncc], func=AF.Copy), reads=[pk], writes=["KCM"])
                ps2 = C.obank[cc % 2]
                pk2 = "obank%d" % (cc % 2)
                mm(P, ps2[0:ncc, 0:64], HS[:, 1, c0:c0 + ncc], w2b[:, 64:128], True, True, ["w2b", "HS"], [pk2])
                P.op("dve", lambda e, ps2=ps2, cc=cc, ncc=ncc: e.tensor_copy(out=VCM[0:ncc, cc, 0:64], in_=ps2[0:ncc, 0:64]), reads=[pk2], writes=["VCM"])
            P.barrier()
            s1b.close()
        P.barrier()
        with contextlib.ExitStack() as s2:
            GB = P.sb("GB", [64, 12, 512], F32, s2)
            GTs = P.sb("GTs", [64, 4, 512], BF16, s2)
            OSn = [P.sb("OSn%d" % i, [64, 4, 512], BF16, s2) for i in range(2)]
            wcf = [P.sb("wcf%d" % i, [128, 4, 128], F32, s2) for i in range(2)]
            wcb = [P.sb("wcb%d" % i, [128, 4, 128], BF16, s2) for i in range(2)]
            rzU = P.sb("rzU", [128, 4], F32, s2)
            imp = P.sb("imp", [128, nsel], F32, s2)
            imp2 = P.sb("imp2", [128, nsel], F32, s2)
            m8a = P.sb("m8a", [128, 8], F32, s2)
            m8b = P.sb("m8b", [128, 8], F32, s2)
            zc = P.sb("zc", [64, 4, 128], F32, s2)
            tq = P.sb("tqn", [64, 4, 128], F32, s2)
            acc = P.sb("accn", [64, 4, 128], F32, s2)
            wci = 0
            for qb in range(NB):
                seg, qi = qb // 4, qb % 4
                qs = slice(qb * 128, (qb + 1) * 128)
                ls = slice(qi * 128, (qi + 1) * 128)
                osn = OSn[seg % 2]
                osk = "OSn%d" % (seg % 2)
                if qi == 0:
                    P.dma("sp", lambda e, seg=seg: e.dma_start(out=GB[:], in_=bass.AP(tensor=GLd.tensor, offset=seg * 512, ap=[[0, 64], [S, 12], [1, 512]])),
                          reads=["GLd"], writes=["GB"], pool="l", npool=6)
                    P.dma("pool", lambda e, seg=seg: e.dma_start(out=GTs[:], in_=PFb[PFB_ROW["ng0"]:PFB_ROW["ng0"] + 256, seg * 512:(seg + 1) * 512].rearrange("(h d) t -> d h t", d=64)),
                          reads=["PFb"], writes=["GTs"], pool="l2", npool=6)
                qrhs = QN[0:64, :, qs]
                tiles = []
                for cc in range(NCC):
                    c0 = cc * 128
                    if qb * 128 + 127 < 16 * c0 + 31:
                        continue
                    wf, wb = wcf[wci % 2], wcb[wci % 2]
                    wfk, wbk = "wcf%d" % (wci % 2), "wcb%d" % (wci % 2)
                    wci += 1
                    off = qb * 128 - 16 * c0 - 2063 + GC_OFF
                    P.dma("sp", lambda e, wf=wf, off=off: e.dma_start(out=wf[:], in_=bass.AP(tensor=GALLd.tensor, offset=off, ap=[[16, 128], [LALL, 4], [1, 128]])),
                          reads=["GALLd"], writes=[wfk], pool="h", npool=2)
                    P.op("pool", lambda e, wf=wf, wb=wb: e.tensor_copy(out=wb[:], in_=wf[:]), reads=[wfk], writes=[wbk])
                    tiles.append(dict(lhsT=KCM[:, c0:c0 + 128], rhs=qrhs, rk=["KCM", "QNq"], extra=(C.Jb[:], wb[:], ["Jb", wbk]), v=VCM[:, cc, :], vk=["VCM"], cc=cc))
                ob = C.obank[C.obi % 2]
                obk = "obank%d" % (C.obi % 2)
                C.obi += 1
                spi0 = C.spi
                attn_tiles(P, C, tiles, ob[:], obk, 512)
                pt = C.ptile[spi0 % 2]
                ptk = "ptile%d" % (spi0 % 2)
                pu = C.psb[qb % 2]
                puk = "psb%d" % (qb % 2)
                for h in range(4):
                    for j, t in enumerate(tiles):
                        mm(P, pu[:, h * 72:h * 72 + nsel + 1], pt[:, j * 512 + h * 128:j * 512 + (h + 1) * 128], OVL[:, t["cc"], :], j == 0, j == len(tiles) - 1,
                           [ptk, "OVL"], [puk])
                pu3 = pu[:, 0:288].rearrange("p (h c) -> p h c", c=72)
                P.op("dve", lambda e, pu3=pu3: e.tensor_scalar(out=rzU[:], in0=pu3[:, :, nsel], scalar1=TINY, scalar2=None, op0=ALU.max), reads=[puk], writes=["rzU"])
                P.op("dve", lambda e: e.reciprocal(out=rzU[:], in_=rzU[:]), reads=["rzU"], writes=["rzU"])
                P.op("dve", lambda e, pu3=pu3: e.tensor_scalar(out=imp[:], in0=pu3[:, 0, 0:nsel], scalar1=rzU[:, 0:1], scalar2=None, op0=ALU.mult), reads=[puk, "rzU"], writes=["imp"])
                for h in range(1, 4):
                    P.op("dve", lambda e, pu3=pu3, h=h: e.scalar_tensor_tensor(out=imp[:], in0=pu3[:, h, 0:nsel], scalar=rzU[:, h:h + 1], in1=imp[:], op0=ALU.mult, op1=ALU.add),
                         reads=[puk, "rzU", "imp"], writes=["imp"])
                P.op("dve", lambda e, qb=qb: e.tensor_tensor(out=imp[:], in0=imp[:], in1=FA[:, qb, :], op=ALU.mult), reads=["imp", "FA"], writes=["imp"])
                P.op("dve", lambda e, qb=qb: e.tensor_tensor(out=imp[:], in0=imp[:], in1=FB[:, qb, :], op=ALU.add), reads=["imp", "FB"], writes=["imp"])
                P.op("dve", lambda e: e.max(out=m8a[:], in_=imp[:]), reads=["imp"], writes=["m8a"])
                P.op("dve", lambda e: e.match_replace(out=imp2[:], in_to_replace=m8a[:], in_values=imp[:], imm_value=-3e9), reads=["imp", "m8a"], writes=["imp2"])
                P.op("dve", lambda e: e.max(out=m8b[:], in_=imp2[:]), reads=["imp2"], writes=["m8b"])
                P.op("dve", lambda e: e.tensor_scalar(out=imp2[:], in0=imp[:], scalar1=m8b[:, 7:8], scalar2=None, op0=ALU.is_ge), reads=["imp", "m8b"], writes=["imp2"])
                P.op("dve", lambda e: e.tensor_scalar(out=imp2[:], in0=imp2[:], scalar1=1.0, scalar2=-NEG, op0=ALU.subtract, op1=ALU.mult), reads=["imp2"], writes=["imp2"])
                pm = C.psb[(qb + 1) % 2]
                pmk = "psb%d" % ((qb + 1) % 2)
                P.op("pe", lambda e, pm=pm: e.transpose(out=pm[0:nsel, 0:128], in_=imp2[:], identity=C.identf[:]), reads=["imp2", "identf"], writes=[pmk])
                P.op("act", lambda e, pm=pm, qs=qs: e.activation(out=QN[64:64 + nsel, :, qs], in_=bc(pm[0:nsel, 0:128].unsqueeze(1), [nsel, 4, 128]), func=AF.Copy),
                     reads=[pmk], writes=["QNm"])

                def epilogue(ob, obk, br, first, ls=ls):
                    o3 = ob[0:64, :].rearrange("p (h t) -> p h t", t=128)
                    z3 = ob[64:128, :].rearrange("p (h t) -> p h t", t=128)
                    P.op("dve", lambda e: e.tensor_scalar(out=zc[:], in0=z3, scalar1=TINY, scalar2=None, op0=ALU.max), reads=[obk], writes=["zc"])
                    P.op("dve", lambda e: e.reciprocal(out=zc[:], in_=zc[:]), reads=["zc"], writes=["zc"])
                    P.op("pool", lambda e: e.tensor_tensor(out=zc[:], in0=zc[:], in1=GB[:, br * 4:(br + 1) * 4, ls], op=ALU.mult), reads=["zc", "GB"], writes=["zc"])
                    if first:
                        P.op("dve", lambda e: e.tensor_tensor(out=acc[:], in0=o3, in1=zc[:], op=ALU.mult), reads=[obk, "zc"], writes=["accn"])
                    else:
                        P.op("dve", lambda e: e.tensor_tensor(out=tq[:], in0=o3, in1=zc[:], op=ALU.mult), reads=[obk, "zc"], writes=["tqn"])
                        P.op("pool", lambda e: e.tensor_tensor(out=acc[:], in0=acc[:], in1=tq[:], op=ALU.add), reads=["accn", "tqn"], writes=["accn"])

                epilogue(ob, obk, 0, True)
                tiles = []
                for kb in range(qb + 1):
                    t = dict(lhsT=KS[:, kb * 128:(kb + 1) * 128], rhs=QN[:, :, qs], rk=["KS", "QNq", "QNm"], extra=None, v=VS[:, kb, :], vk=["VS"])
                    if qb - kb <= 1:
                        off = (qb - kb) * 128
                        t["extra"] = (C.Jb[:], WRs[:, :, off:off + 128], ["Jb", "WRs"])
                    tiles.append(t)
                ob = C.obank[C.obi % 2]
                obk = "obank%d" % (C.obi % 2)
                C.obi += 1
                attn_tiles(P, C, tiles, ob[:], obk, 512)
                epilogue(ob, obk, 1, False)
                tiles = []
                for kb in range(max(0, qb - 4), qb + 1):
                    t = dict(lhsT=KW[:, kb * 128:(kb + 1) * 128], rhs=qrhs, rk=["KW", "QNq"], extra=None, v=VW[:, kb, :], vk=["VW"])
                    off = (qb - kb) * 128
                    if off in (0, 128, 512):
                        t["extra"] = (C.Jb[:], WRw[:, :, off:off + 128], ["Jb", "WRw"])
                    tiles.append(t)
                ob = C.obank[C.obi % 2]
                obk = "obank%d" % (C.obi % 2)
                C.obi += 1
                attn_tiles(P, C, tiles, ob[:], obk, 512)
                epilogue(ob, obk, 2, False)
                P.op("pool", lambda e, osn=osn, ls=ls: e.tensor_tensor(out=osn[:, :, ls], in0=acc[:], in1=GTs[:, :, ls], op=ALU.mult), reads=["accn", "GTs"], writes=[osk])
                if qi == 3:
                    P.dma("sp", lambda e, osn=osn, seg=seg: e.dma_start(out=OT[512:768, seg * 512:(seg + 1) * 512].rearrange("(h d) t -> d h t", d=64), in_=osn[:]),
                          reads=[osk], writes=["OT"], pool="o", npool=4)
        P.barrier()
    P.barrier()


def phase_outproj(P, C, St, xT, OTf, wo, xTo):
    NQC = St // 512
    with contextlib.ExitStack() as st:
        Wo = P.sb("Wo", [128, 16, 1024], BF16, st)
        wos = [P.sb("wos%d" % i, [128, 16, 128], F32, st) for i in range(2)]
        OA = [P.sb("OA%d" % i, [128, 16, 512], BF16, st) for i in range(2)]
        XA = [P.sb("XA%d" % i, [128, 8, 512], F32, st) for i in range(2)]
        wv = wo.rearrange("(k p) c -> p k c", p=128)
        for fg in range(8):
            ws = wos[fg % 2]
            wk = "wos%d" % (fg % 2)
            P.dma("sp" if fg % 2 == 0 else "pool", lambda e, ws=ws, fg=fg: e.dma_start(out=ws[:], in_=wv[:, :, fg * 128:(fg + 1) * 128]), writes=[wk], pool="w", npool=2)
            P.op("pool" if fg % 2 == 0 else "act",
                 (lambda e, ws=ws, fg=fg: e.tensor_copy(out=Wo[:, :, fg * 128:(fg + 1) * 128], in_=ws[:])) if fg % 2 == 0 else
                 (lambda e, ws=ws, fg=fg: e.activation(out=Wo[:, :, fg * 128:(fg + 1) * 128], in_=ws[:], func=AF.Copy)),
                 reads=[wk], writes=["Wo"])
        ov = OTf.rearrange("(k p) t -> p k t", p=128)
        xv = xT.rearrange("(k p) t -> p k t", p=128)
        xov = xTo.rearrange("(k p) t -> p k t", p=128)
        for qc in range(NQC):
            qs = slice(qc * 512, (qc + 1) * 512)
            oa, xa = OA[qc % 2], XA[qc % 2]
            ok, xk = "OA%d" % (qc % 2), "XA%d" % (qc % 2)
            P.dma("sp", lambda e, oa=oa, qs=qs: e.dma_start(out=oa[:], in_=ov[:, :, qs]), reads=["OTf"], writes=[ok], pool="x", npool=2)
            P.dma("pool", lambda e, xa=xa, qs=qs: e.dma_start(out=xa[:], in_=xv[:, :, qs]), reads=["xTin"], writes=[xk], pool="x2", npool=2)
            for fg in range(8):
                ps = C.psb[fg % 2]
                pk = "psb%d" % (fg % 2)
                for k in range(16):
                    mm(P, ps[:], Wo[:, k, fg * 128:(fg + 1) * 128], oa[:, k, :], k == 0, k == 15, ["Wo", ok], [pk])
                P.op("dve", lambda e, ps=ps, xa=xa, fg=fg: e.tensor_tensor(out=xa[:, fg, :], in0=ps[:], in1=xa[:, fg, :], op=ALU.add), reads=[pk, xk], writes=[xk])
            P.dma("sp", lambda e, xa=xa, qs=qs: e.dma_start(out=xov[:, :, qs], in_=xa[:]), reads=[xk], writes=["xTout"], pool="o", npool=4)
    P.barrier()


def phase_final(P, C, St, xT, fg_, outT):
    NQC = St // 512
    with contextlib.ExitStack() as st:
        xs = [P.sb("fx%d" % i, [128, 8, 512], F32, st) for i in range(2)]
        sq = P.sb("fsq", [128, 8, 512], BF16, st)
        rt = P.sb("frt", [128, 512], F32, st)
        gcol = P.sb("fgc", [128, 8], F32, st)
        P.dma("sp", lambda e: e.dma_start(out=gcol[:], in_=fg_), writes=["fgc"], pool="l", npool=6)
        xv = xT.rearrange("(k p) t -> p k t", p=128)
        ov = outT.rearrange("(k p) t -> p k t", p=128)
        for qc in range(NQC):
            qs = slice(qc * 512, (qc + 1) * 512)
            xb = xs[qc % 2]
            xk = "fx%d" % (qc % 2)
            P.dma("sp", lambda e, xb=xb, qs=qs: e.dma_start(out=xb[:], in_=xv[:, :, qs]), reads=["xTout"], writes=[xk], pool="x", npool=2)
            P.op("act", lambda e, xb=xb: e.activation(out=sq[:], in_=xb[:], func=AF.Square), reads=[xk], writes=["fsq"])
            ps = C.psb[qc % 2]
            pk = "psb%d" % (qc % 2)
            for k in range(8):
                mm(P, ps[:], C.onesb[:, 0:128], sq[:, k, :], k == 0, k == 7, ["fsq", "onesb"], [pk])
            P.op("act", lambda e, ps=ps: e.activation(out=rt[:], in_=ps[:], func=AF.Sqrt, bias=C.epsc[:, 0:1], scale=1.0 / D_MODEL), reads=[pk, "epsc"], writes=["frt"])
            P.op("dve", lambda e: e.reciprocal(out=rt[:], in_=rt[:]), reads=["frt"], writes=["frt"])
            for k in range(8):
                P.op("dve", lambda e, xb=xb, k=k: e.scalar_tensor_tensor(out=xb[:, k, :], in0=xb[:, k, :], scalar=gcol[:, k:k + 1], in1=rt[:], op0=ALU.mult, op1=ALU.mult),
                     reads=[xk, "frt", "fgc"], writes=[xk])
            P.dma("sp", lambda e, xb=xb, qs=qs: e.dma_start(out=ov[:, :, qs], in_=xb[:]), reads=[xk], writes=["outT"], pool="o", npool=4)
    P.barrier()


def make_consts2():
    c = make_consts()
    c["cIf"] = np.eye(128, dtype=np.float32)
    c["cIb"] = np.eye(128, dtype=np.float32).astype(ml_dtypes.bfloat16)
    j = np.arange(128)[:, None]
    l = np.arange(128)[None, :]
    c["cU"] = (j <= l).astype(np.float32)
    c["cSH"] = np.where(l == j - 1, NEG, 0.0).astype(np.float32)
    return c


CONST_SPECS = [("cJ", [128, 128], BF16), ("cWRc", [128, 896], BF16), ("cIf", [128, 128], F32), ("cIb", [128, 128], BF16),
               ("cU", [128, 128], F32), ("cSH", [128, 128], F32)]


def setup_ctx2(P, nc, cd):
    C = setup_ctx(P, nc, cd["cJ"], cd["cWRc"])
    C.identf = P.sb("identf", [128, 128], F32)
    C.identb = P.sb("identb", [128, 128], BF16)
    C.Umat = P.sb("Umat", [128, 128], F32)
    C.SHm = P.sb("SHm", [128, 128], F32)
    C.onesf = P.sb("onesf", [128, 128], F32)
    P.op("pool", lambda e: e.memset(C.onesf[:], 1.0), writes=["onesf"])
    for nm, t in (("cIf", C.identf), ("cIb", C.identb), ("cU", C.Umat), ("cSH", C.SHm)):
        P.dma("sp", lambda e, nm=nm, t=t: e.dma_start(out=t[:], in_=cd[nm]), writes=[nm])
    P.barrier()
    return C


def layer_specs(S, pfx):
    nsel = S // 64
    NB = S // 128
    NCC = (S // 16 - 1 + 127) // 128
    LALL = S + 4080 + 383 + 767
    sp = [("wl", [1024, 3220], F32), ("gl", [128, 8], F32), ("fb", [4, 1], F32), ("wkv", [1024, 512], F32), ("mg", [128, 8], F32),
          ("cw", [128, 4, 4], F32), ("cb", [128, 4], F32), ("dtb", [4, 1], F32), ("alog", [4, 1], F32), ("dsk", [128, 256], F32), ("ngs", [128, 2], F32),
          ("nGALL", [4, LALL], F32), ("nt31", [4, 1], F32), ("nE", [64, S], BF16), ("nOVL", [128, NCC, nsel + 1], BF16),
          ("nFA", [128, NB, nsel], F32), ("nFB", [128, NB, nsel], F32), ("npe", [128, 32], F32), ("nw1", [128, 32, 128], F32), ("nw2", [128, 128], F32)]
    return [(pfx + n, s, d) for n, s, d in sp]


def layer_host(inp, l, hf, S, pfx):
    cols = core_cols(hf)
    d = {}
    d["wl"] = np.ascontiguousarray(inp["w_in"][l][:, cols])
    d["gl"] = np.ascontiguousarray(inp["norm_g"][l].reshape(8, 128).T)
    d["fb"] = inp["fox_f_bias"][l][4 * hf:4 * hf + 4].reshape(4, 1).copy()
    wk = inp["w_mem_kv"][l]
    d["wkv"] = np.ascontiguousarray(np.concatenate([wk[:, 256 * hf:256 * hf + 256], wk[:, 512 + 256 * hf:512 + 256 * hf + 256]], 1))
    d["mg"] = np.ascontiguousarray(inp["mem_norm_g"][l].reshape(8, 128).T)
    ch = np.concatenate([256 * hf + np.arange(256), 512 + 128 * hf + np.arange(128), 768 + 128 * hf + np.arange(128)])
    d["cw"] = np.ascontiguousarray(inp["ssm_conv_w"][l][:, ch].reshape(4, 4, 128).transpose(2, 1, 0))
    d["cb"] = np.ascontiguousarray(inp["ssm_conv_b"][l][ch].reshape(4, 128).T)
    d["dtb"] = inp["ssm_dt_bias"][l][4 * hf:4 * hf + 4].reshape(4, 1).copy()
    d["alog"] = inp["ssm_a_log"][l][4 * hf:4 * hf + 4].reshape(4, 1).copy()
    d["dsk"] = np.ascontiguousarray(np.broadcast_to(np.repeat(inp["ssm_d"][l][4 * hf:4 * hf + 4], 64)[None, :], (128, 256)))
    d["ngs"] = np.ascontiguousarray(inp["ssm_norm_g"][l][256 * hf:256 * hf + 256].reshape(2, 128).T)
    d.update(nsa_host(S, inp["rel_bias_table"][:, 4 * hf:4 * hf + 4], inp["nsa_cmp_pe"][l], inp["nsa_cmp_w1"][l], inp["nsa_cmp_w2"][l]))
    return {pfx + k: v for k, v in d.items()}


def mixers(P, C, S, xT, memT, L, scr, OT):
    phase_inproj(P, C, S, xT, L["wl"], L["gl"], scr["PFb"], scr["PFf"], scr["VT"])
    phase_fox(P, C, S, scr["PFb"], scr["PFf"], scr["VT"], L["fb"], OT)
    phase_mem(P, C, S, scr["PFb"], memT, L["wkv"], L["mg"], OT)
    phase_ssd(P, C, S, scr["PFb"], scr["PFf"], L["cw"], L["cb"], L["dtb"], L["alog"], L["dsk"], L["ngs"], OT)
    phase_nsa(P, C, S, scr["PFb"], scr["PFf"], scr["VT"], L, scr["GALLd"], scr["GLd"], OT)


def alloc_scratch(nc, S, kind="Internal"):
    dt = lambda n, s, d: nc.dram_tensor(n, s, d, kind=kind).ap()
    LALL = S + 4080 + 383 + 767
    return {"PFb": dt("PFb", [18 * 128, S], BF16), "PFf": dt("PFf", [5 * 128, S], F32), "VT": dt("VT", [S, 6, 128], BF16),
            "GALLd": dt("GALLd", [4, LALL], F32), "GLd": dt("GLd", [12, S], F32)}


SEQ = 4096
NCORES = 8


def _dt(nc):
    return lambda n, s, d, k="ExternalInput": nc.dram_tensor(n, s, d, kind=k).ap()


def build_L1(S):
    nc = bass.Bass("TRN2", target_bir_lowering=False)
    dt = _dt(nc)
    xT = dt("xT", [1024, S], F32)
    memT = dt("memT", [1024, 256], F32)
    cd = {n: dt(n, s, d) for n, s, d in CONST_SPECS}
    L = {n: dt(n, s, d) for n, s, d in layer_specs(S, "")}
    scr = alloc_scratch(nc, S)
    OT = dt("OT", [1024, S], BF16, "ExternalOutput")
    P = Prog(nc)
    C = setup_ctx2(P, nc, cd)
    mixers(P, C, S, xT, memT, L, scr, OT)
    P.wait_all("sp")
    P.emit()
    P.close()
    return nc


def build_L2(S):
    nc = bass.Bass("TRN2", target_bir_lowering=False)
    dt = _dt(nc)
    xT = dt("xT", [1024, S], F32)
    memT = dt("memT", [1024, 256], F32)
    OTf = dt("OTf", [2048, S], BF16)
    wo = dt("wo", [2048, 1024], F32)
    cd = {n: dt(n, s, d) for n, s, d in CONST_SPECS}
    L = {n: dt(n, s, d) for n, s, d in layer_specs(S, "")}
    scr = alloc_scratch(nc, S)
    x1T = dt("x1T", [1024, S], F32, "ExternalOutput")
    OT = dt("OT", [1024, S], BF16, "ExternalOutput")
    P = Prog(nc)
    C = setup_ctx2(P, nc, cd)
    phase_outproj(P, C, S, xT, OTf, wo, x1T)
    mixers(P, C, S, x1T, memT, L, scr, OT)
    P.wait_all("sp")
    P.emit()
    P.close()
    return nc


def build_L3(St):
    nc = bass.Bass("TRN2", target_bir_lowering=False)
    dt = _dt(nc)
    xT = dt("xT", [1024, St], F32)
    OTf = dt("OTf", [2048, St], BF16)
    wo = dt("wo", [2048, 1024], F32)
    fg = dt("fg", [128, 8], F32)
    cd = {n: dt(n, s, d) for n, s, d in CONST_SPECS}
    x2T = nc.dram_tensor("x2T", [1024, St], F32, kind="Internal").ap()
    outT = dt("outT", [1024, St], F32, "ExternalOutput")
    P = Prog(nc)
    C = setup_ctx2(P, nc, cd)
    phase_outproj(P, C, St, xT, OTf, wo, x2T)
    phase_final(P, C, St, x2T, fg, outT)
    P.wait_all("sp")
    P.emit()
    P.close()
    return nc


def wo_perm(w):
    idx = np.concatenate([512 * m + 256 * hf + np.arange(256) for hf in range(2) for m in range(4)])
    return np.ascontiguousarray(w[idx])


def build_fused(S):
    nc = bass.Bass("TRN2", target_bir_lowering=False)
    dt = _dt(nc)
    xT = dt("xT", [1024, S], F32)
    memT = dt("memT", [1024, 256], F32)
    fg = dt("fg", [128, 8], F32)
    cd = {n: dt(n, s, d) for n, s, d in CONST_SPECS}
    LL = {}
    for l in range(2):
        for hf in range(2):
            pfx = "l%dh%d_" % (l, hf)
            LL[(l, hf)] = {n[len(pfx):]: dt(n, s, d) for n, s, d in layer_specs(S, pfx)}
    wos = [dt("wo%d" % l, [2048, 1024], F32) for l in range(2)]
    scr = alloc_scratch(nc, S)
    OTf = nc.dram_tensor("OTf", [2048, S], BF16, kind="Internal").ap()
    xa = nc.dram_tensor("xa", [1024, S], F32, kind="Internal").ap()
    xb = nc.dram_tensor("xb", [1024, S], F32, kind="Internal").ap()
    outT = dt("outT", [1024, S], F32, "ExternalOutput")
    P = Prog(nc)
    C = setup_ctx2(P, nc, cd)
    xin = xT
    for l in range(2):
        for hf in range(2):
            mixers(P, C, S, xin, memT, LL[(l, hf)], scr, OTf[hf * 1024:(hf + 1) * 1024, :])
        xout = xa if l == 0 else xb
        phase_outproj(P, C, S, xin, OTf, wos[l], xout)
        xin = xout
    phase_final(P, C, S, xin, fg, outT)
    P.wait_all("sp")
    P.emit()
    P.close()
    return nc


def kernel(**inputs):
    inp = {k: np.asarray(v) for k, v in inputs.items()}
    S = inp["x"].shape[1]
    B = inp["x"].shape[0]
    consts = make_consts2()
    fgh = np.ascontiguousarray(inp["final_norm_g"].reshape(8, 128).T)
    shared = dict(consts)
    shared["fg"] = fgh
    for l in range(2):
        shared["wo%d" % l] = wo_perm(inp["w_out"][l])
        for hf in range(2):
            shared.update(layer_host(inp, l, hf, S, "l%dh%d_" % (l, hf)))
    ins = []
    for b in range(B):
        d = {"xT": np.ascontiguousarray(inp["x"][b].T), "memT": np.ascontiguousarray(inp["mem"][b].T)}
        d.update(shared)
        ins.append(d)
    res = run_bass_kernel_spmd(build_fused(S), ins, core_ids=list(range(B))).results
    out = np.empty((B, S, 1024), np.float32)
    for b in range(B):
        out[b] = np.asarray(res[b]["outT"]).T
    return out
```

```python
import contextlib
import math
import numpy as np
import ml_dtypes
import concourse.bass as bass
import concourse.mybir as mybir
from concourse.bass_utils import run_bass_kernel_spmd

F32 = mybir.dt.float32
BF16 = mybir.dt.bfloat16
AF = mybir.ActivationFunctionType
ALU = mybir.AluOpType
AX = mybir.AxisListType
ENGS = ("pe", "act", "dve", "pool", "sp")
NEG = -30000.0
D_MODEL = 1024
IN_COLS = 6440
EPS = 1e-6


class Prog:
    def __init__(self, nc):
        self.nc = nc
        self.es = contextlib.ExitStack()
        self.streams = {e: [] for e in ENGS}
        self.cnt = {}
        self.sems = {}
        self.seen = {e: {} for e in ENGS}
        self.bufs = {}
        self.pools = {}
        for e in ENGS:
            self._newsem(e)

    def _newsem(self, key):
        self.sems[key] = self.es.enter_context(self.nc.semaphore("s_" + key))
        self.cnt[key] = 0

    def sb(self, name, shape, dtype, stack=None):
        self.uid = getattr(self, "uid", 0) + 1
        return (stack or self.es).enter_context(self.nc.sbuf_tensor("%s_u%d" % (name, self.uid), list(shape), dtype))

    def ps(self, name, shape, dtype=F32):
        return self.es.enter_context(self.nc.psum_tensor(name, list(shape), dtype))

    def _need(self, eng, reads, writes):
        need = {}

        def add(ev, raw):
            if ev is None:
                return
            k, v = ev
            if k == eng and (eng == "pe" or not raw):
                return
            if need.get(k, 0) < v:
                need[k] = v

        for b in reads:
            st = self.bufs.get(b)
            if st:
                add(st[0], True)
        for b in writes:
            st = self.bufs.get(b)
            if st:
                add(st[0], False)
                for ev in st[1]:
                    add(ev, False)
        for k, v in need.items():
            if self.seen[eng].get(k, 0) < v:
                self.seen[eng][k] = v
                self.streams[eng].append(("wait", k, v))

    def _mark(self, ev, reads, writes):
        for b in reads:
            st = self.bufs.setdefault(b, [None, []])
            st[1].append(ev)
        for b in writes:
            self.bufs[b] = [ev, []]

    def op(self, eng, fn, reads=(), writes=()):
        self._need(eng, reads, writes)
        self.cnt[eng] += 1
        ev = (eng, self.cnt[eng])
        self.streams[eng].append(("op", fn, eng, 1))
        self._mark(ev, reads, writes)
        return ev

    def dma(self, eng, fn, reads=(), writes=(), pool="d", npool=6):
        pl = self.pools.setdefault(pool, {"n": 0, "keys": []})
        i = pl["n"] % npool
        pl["n"] += 1
        if i >= len(pl["keys"]):
            key = "dma_%s_%d" % (pool, i)
            self._newsem(key)
            pl["keys"].append(key)
        key = pl["keys"][i]
        prev = self.cnt[key]
        if prev > 0 and self.seen[eng].get(key, 0) < prev:
            self.seen[eng][key] = prev
            self.streams[eng].append(("wait", key, prev))
        self._need(eng, reads, writes)
        self.cnt[key] += 16
        ev = (key, self.cnt[key])
        self.streams[eng].append(("op", fn, key, 16))
        self._mark(ev, reads, writes)
        return ev

    def wait_all(self, eng):
        for k, v in self.cnt.items():
            if k != eng and v > 0 and self.seen[eng].get(k, 0) < v:
                self.seen[eng][k] = v
                self.streams[eng].append(("wait", k, v))

    def barrier(self):
        for e in ENGS:
            self.wait_all(e)
        self.bufs = {}

    def ninstr(self):
        return sum(1 for e in ENGS for it in self.streams[e] if it[0] == "op")

    def emit(self):
        nc = self.nc
        sems = self.sems
        streams = self.streams

        def run(engobj, name):
            for it in streams[name]:
                if it[0] == "wait":
                    engobj.wait_ge(sems[it[1]], it[2])
                else:
                    ins = it[1](engobj)
                    ins.then_inc(sems[it[2]], it[3])

        with nc.Block() as block:
            @block.tensor
            def _(e):
                run(e, "pe")

            @block.scalar
            def _(e):
                run(e, "act")

            @block.vector
            def _(e):
                run(e, "dve")

            @block.gpsimd
            def _(e):
                run(e, "pool")

            @block.sync
            def _(e):
                run(e, "sp")

    def close(self):
        self.es.close()


FG = ["fq0", "fq1", "fk0", "fk1", "fg0", "fg1", "sz0", "sz1", "sx0", "sx1", "sB", "sC",
      "small", "nq0", "nq1", "ncv", "nkk", "ng0", "ng1", "mq0", "mq1", "mg0", "mg1"]
PFB_NAMES = ["fq0", "fq1", "fk0", "fk1", "fg0", "fg1", "sz0", "sz1", "nq0", "nq1", "ncv", "nkk",
             "ng0", "ng1", "mq0", "mq1", "mg0", "mg1"]
PFF_NAMES = ["sx0", "sx1", "sB", "sC", "small"]
PFB_ROW = {n: 128 * i for i, n in enumerate(PFB_NAMES)}
PFF_ROW = {n: 128 * i for i, n in enumerate(PFF_NAMES)}
NFCOL = 22 * 128 + 20
NTCOL = 384


def core_cols(hf):
    r = np.arange
    FOX, SSM, NSA, MEM = 0, 2056, 3600, 5416
    c = {}
    c["fq"] = FOX + 256 * hf + r(256)
    c["fk"] = FOX + 512 + 256 * hf + r(256)
    c["fv"] = FOX + 1024 + 256 * hf + r(256)
    c["fg"] = FOX + 1536 + 256 * hf + r(256)
    c["ff"] = FOX + 2048 + 4 * hf + r(4)
    c["sz"] = SSM + 256 * hf + r(256)
    c["sx"] = SSM + 512 + 256 * hf + r(256)
    c["sB"] = SSM + 1024 + 128 * hf + r(128)
    c["sC"] = SSM + 1280 + 128 * hf + r(128)
    c["sdt"] = SSM + 1536 + 4 * hf + r(4)
    c["nq"] = NSA + 256 * hf + r(256)
    c["kc"] = NSA + 512 + 64 * hf + r(64)
    c["vc"] = NSA + 640 + 64 * hf + r(64)
    c["ks"] = NSA + 768 + 64 * hf + r(64)
    c["vs"] = NSA + 896 + 64 * hf + r(64)
    c["kw"] = NSA + 1024 + 64 * hf + r(64)
    c["vw"] = NSA + 1152 + 64 * hf + r(64)
    c["gl"] = NSA + 1280 + np.array([br * 8 + hf * 4 + rr for br in range(3) for rr in range(4)])
    c["ng"] = NSA + 1304 + 256 * hf + r(256)
    c["mq"] = MEM + 256 * hf + r(256)
    c["mg"] = MEM + 512 + 256 * hf + r(256)
    order = ["fq", "fk", "fg", "sz", "sx", "sB", "sC", "ff", "sdt", "gl", "nq", "kc", "vc", "ks", "kw",
             "ng", "mq", "mg", "fv", "vs", "vw"]
    return np.concatenate([c[k] for k in order])


def t5_bucket(d):
    d = np.asarray(d)
    n = np.maximum(d, 0)
    nf = np.maximum(n, 1).astype(np.float32)
    large = 16 + (np.log(nf / 16) / math.log(128 / 16) * 16).astype(np.int32)
    large = np.minimum(large, 31)
    return np.where(n < 16, n, large)


class Ctx:
    pass


def mm(P, out, lhsT, rhs, start, stop, reads, writes):
    P.op("pe", lambda e: e.matmul(out, lhsT=lhsT, rhs=rhs, start=start, stop=stop), reads=reads, writes=writes)


def phase_inproj(P, C, S, xT, wl, gl, PFb, PFf, VT):
    nc = P.nc
    NQC = S // 512
    with contextlib.ExitStack() as st:
        hT = P.sb("hT", [128, 8, S], BF16, st)
        xs = [P.sb("xst%d" % i, [128, 8, 512], F32, st) for i in range(2)]
        sq = P.sb("sq", [128, 8, 512], BF16, st)
        rt = P.sb("rt", [128, 512], F32, st)
        gcol = P.sb("gcol", [128, 8], F32, st)
        wst = [P.sb("wst%d" % i, [128, 8, 128], F32, st) for i in range(2)]
        wbf = [P.sb("wbf%d" % i, [128, 8, 128], BF16, st) for i in range(2)]
        wtst = P.sb("wtst", [128, 8, NTCOL], F32, st)
        wtb = P.sb("wtb", [128, 8, NTCOL], BF16, st)
        stb = [P.sb("stb%d" % i, [128, S], BF16, st) for i in range(2)]
        stf = P.sb("stf", [128, S], F32, st)
        vst = [P.sb("vst%d" % i, [128, 6, 128], BF16, st) for i in range(2)]
        P.dma("sp", lambda e: e.dma_start(out=gcol[:], in_=gl), writes=["gcol"])
        for i in range(2):
            P.op("pool", lambda e, i=i: e.memset(vst[i][:], 1.0), writes=["vst%d" % i])
        xv = xT.rearrange("(k p) t -> p k t", p=128)
        for qc in range(NQC):
            xb = xs[qc % 2]
            xk = "xst%d" % (qc % 2)
            P.dma("sp", lambda e, xb=xb, qc=qc: e.dma_start(out=xb[:], in_=xv[:, :, qc * 512:(qc + 1) * 512]),
                  writes=[xk], pool="x", npool=2)
            P.op("act", lambda e, xb=xb: e.activation(out=sq[:], in_=xb[:], func=AF.Square), reads=[xk], writes=["sq"])
            ps = C.psb[qc % 2]
            pk = "psb%d" % (qc % 2)
            for k in range(8):
                mm(P, ps[:], C.onesb[:, 0:128], sq[:, k, :], k == 0, k == 7, ["sq", "onesb"], [pk])
            P.op("act", lambda e, ps=ps: e.activation(out=rt[:], in_=ps[:], func=AF.Sqrt, bias=C.epsc[:, 0:1], scale=1.0 / D_MODEL),
                 reads=[pk, "epsc"], writes=["rt"])
            P.op("dve", lambda e: e.reciprocal(out=rt[:], in_=rt[:]), reads=["rt"], writes=["rt"])
            for k in range(8):
                P.op("dve",
                     lambda e, xb=xb, k=k, qc=qc: e.scalar_tensor_tensor(
                         out=hT[:, k, qc * 512:(qc + 1) * 512], in0=xb[:, k, :], scalar=gcol[:, k:k + 1], in1=rt[:],
                         op0=ALU.mult, op1=ALU.mult),
                     reads=[xk, "rt", "gcol"], writes=["hT"])
        wv = wl.rearrange("(k p) c -> p k c", p=128)
        col = 0
        nb = 0
        for gi, gname in enumerate(FG):
            ncol = 20 if gname == "small" else 128
            ws = wst[gi % 2]
            wb = wbf[gi % 2]
            wk, bk = "wst%d" % (gi % 2), "wbf%d" % (gi % 2)
            P.dma("sp", lambda e, ws=ws, col=col, ncol=ncol: e.dma_start(out=ws[:, :, 0:ncol], in_=wv[:, :, col:col + ncol]),
                  writes=[wk], pool="w", npool=2)
            P.op("pool", lambda e, ws=ws, wb=wb, ncol=ncol: e.tensor_copy(out=wb[:, :, 0:ncol], in_=ws[:, :, 0:ncol]),
                 reads=[wk], writes=[bk])
            isf = gname in PFF_ROW
            if isf:
                stg, sk = stf, "stf"
            else:
                stg, sk = stb[nb % 2], "stb%d" % (nb % 2)
                nb += 1
            for qc in range(NQC):
                ps = C.psb[(gi * NQC + qc) % 2]
                pk = "psb%d" % ((gi * NQC + qc) % 2)
                for k in range(8):
                    mm(P, ps[0:ncol, :], wb[:, k, 0:ncol], hT[:, k, qc * 512:(qc + 1) * 512], k == 0, k == 7, [bk, "hT"], [pk])
                o = stg[0:ncol, qc * 512:(qc + 1) * 512]
                pin = ps[0:ncol, :]
                base = gname[:2]
                eng = "act"
                if base in ("fg", "sz", "ng", "mg"):
                    fn = lambda e, o=o, pin=pin: e.activation(out=o, in_=pin, func=AF.Silu)
                elif base in ("fq", "nq"):
                    fn = lambda e, o=o, pin=pin: e.activation(out=o, in_=pin, func=AF.Copy, scale=0.125)
                elif base == "mq":
                    fn = lambda e, o=o, pin=pin: e.activation(out=o, in_=pin, func=AF.Copy, scale=128.0 ** -0.5)
                else:
                    eng = "dve"
                    fn = lambda e, o=o, pin=pin: e.tensor_copy(out=o, in_=pin)
                P.op(eng, fn, reads=[pk], writes=[sk])
            if isf:
                r0 = PFF_ROW[gname]
                P.dma("sp", lambda e, stg=stg, r0=r0, ncol=ncol: e.dma_start(out=PFf[r0:r0 + ncol, :], in_=stg[0:ncol, :]),
                      reads=[sk], writes=["PFf"], pool="o", npool=4)
            else:
                r0 = PFB_ROW[gname]
                P.dma("sp", lambda e, stg=stg, r0=r0: e.dma_start(out=PFb[r0:r0 + 128, :], in_=stg[:, :]),
                      reads=[sk], writes=["PFb"], pool="o", npool=4)
            col += ncol
        P.dma("sp", lambda e: e.dma_start(out=wtst[:], in_=wv[:, :, col:col + NTCOL]), writes=["wtst"], pool="w", npool=2)
        P.op("pool", lambda e: e.tensor_copy(out=wtb[:], in_=wtst[:]), reads=["wtst"], writes=["wtb"])
        for tt in range(S // 128):
            ps = C.psb[tt % 2]
            pk = "psb%d" % (tt % 2)
            for k in range(8):
                mm(P, ps[:, 0:NTCOL], hT[:, k, tt * 128:(tt + 1) * 128], wtb[:, k, :], k == 0, k == 7, ["wtb", "hT"], [pk])
            vs_ = vst[tt % 2]
            vk = "vst%d" % (tt % 2)
            P.op("act" if tt % 2 == 0 else "dve",
                 (lambda e, vs_=vs_, ps=ps: e.activation(out=vs_[:, :, 0:64], in_=ps[:, 0:NTCOL].rearrange("p (a b) -> p a b", b=64), func=AF.Copy))
                 if tt % 2 == 0 else
                 (lambda e, vs_=vs_, ps=ps: e.tensor_copy(out=vs_[:, :, 0:64], in_=ps[:, 0:NTCOL].rearrange("p (a b) -> p a b", b=64))),
                 reads=[pk], writes=[vk])
            P.dma("sp", lambda e, vs_=vs_, tt=tt: e.dma_start(out=VT[tt * 128:(tt + 1) * 128, :, :], in_=vs_[:]),
                  reads=[vk], writes=["VT"], pool="o", npool=4)
    P.barrier()


def attn_tiles(P, C, tiles, O, okey, N):
    nt = len(tiles)
    groups = [tiles[i:i + 2] for i in range(0, nt, 2)]
    bufs = []

    def qk_exp(grp):
        sp = C.spair[C.spi % 2]
        spk = "spair%d" % (C.spi % 2)
        pt = C.ptile[C.spi % 2]
        ptk = "ptile%d" % (C.spi % 2)
        C.spi += 1
        for j, t in enumerate(grp):
            o = sp[:, j * 512:j * 512 + N]
            ex = t.get("extra")
            mm(P, o, t["lhsT"], t["rhs"], True, ex is None, t["rk"], [spk])
            if ex is not None:
                mm(P, o, ex[0], ex[1], False, True, ex[2], [spk])
        ng = len(grp)
        if N == 512:
            P.op("act", lambda e, sp=sp, pt=pt, ng=ng: e.activation(out=pt[:, 0:ng * 512], in_=sp[:, 0:ng * 512], func=AF.Exp),
                 reads=[spk], writes=[ptk])
        else:
            for j in range(ng):
                P.op("act", lambda e, sp=sp, pt=pt, j=j: e.activation(out=pt[:, j * 512:j * 512 + N], in_=sp[:, j * 512:j * 512 + N], func=AF.Exp),
                     reads=[spk], writes=[ptk])
        return pt, ptk

    def pv(gi, pt, ptk):
        for j, t in enumerate(groups[gi]):
            idx = gi * 2 + j
            mm(P, O, t["v"], pt[:, j * 512:j * 512 + N], idx == 0, idx == nt - 1, [ptk] + t["vk"], [okey])

    prev = None
    for gi, grp in enumerate(groups):
        cur = qk_exp(grp)
        if prev is not None:
            pv(gi - 1, *prev)
        prev = cur
    pv(len(groups) - 1, *prev)


def phase_fox(P, C, S, PFb, PFf, VT, fb, OT):
    NB = S // 128
    NQC = S // 512
    with contextlib.ExitStack() as st:
        QF = P.sb("QF", [68, 4, S], BF16, st)
        KF = P.sb("KF", [68, 4, S], BF16, st)
        VA = P.sb("VA", [128, NB, 4, 128], BF16, st)
        P.op("pool", lambda e: e.memset(QF[64:68, :, :], 1.0), writes=["QF"])
        P.op("pool", lambda e: e.memset(KF[64:68, :, :], 1.0), writes=["KF"])
        for h in range(4):
            g, r0 = h // 2, (h % 2) * 64
            P.dma("sp", lambda e, h=h, g=g, r0=r0: e.dma_start(out=QF[0:64, h, :], in_=PFb[PFB_ROW["fq%d" % g] + r0:PFB_ROW["fq%d" % g] + r0 + 64, :]),
                  reads=["PFb"], writes=["QF"], pool="l", npool=6)
            P.dma("sp", lambda e, h=h, g=g, r0=r0: e.dma_start(out=KF[0:64, h, :], in_=PFb[PFB_ROW["fk%d" % g] + r0:PFB_ROW["fk%d" % g] + r0 + 64, :]),
                  reads=["PFb"], writes=["KF"], pool="l", npool=6)
        P.dma("pool", lambda e: e.dma_start(out=VA[:], in_=VT[:, 0:4, :].rearrange("(b p) h c -> p b h c", p=128)),
              reads=["VT"], writes=["VA"], pool="l2", npool=6)
        with contextlib.ExitStack() as s1:
            f4 = P.sb("f4", [4, S], F32, s1)
            t1 = P.sb("t1", [4, S], F32, s1)
            t2 = P.sb("t2", [4, S], F32, s1)
            cc = P.sb("cc", [4, S], F32, s1)
            hb = [P.sb("hb%d" % i, [4, S], BF16, s1) for i in range(4)]
            fbc = P.sb("fbc", [4, 1], F32, s1)
            one4 = P.sb("one4", [4, 512], F32, s1)
            P.op("pool", lambda e: e.memset(one4[:], 1.0), writes=["one4"])
            P.dma("sp", lambda e: e.dma_start(out=f4[:], in_=PFf[PFF_ROW["small"]:PFF_ROW["small"] + 4, :]), reads=["PFf"], writes=["f4"], pool="l", npool=6)
            P.dma("sp", lambda e: e.dma_start(out=fbc[:], in_=fb), writes=["fbc"], pool="l", npool=6)
            P.op("dve", lambda e: e.tensor_scalar(out=f4[:], in0=f4[:], scalar1=fbc[:, 0:1], scalar2=None, op0=ALU.add), reads=["f4", "fbc"], writes=["f4"])
            P.op("act", lambda e: e.activation(out=t1[:], in_=f4[:], func=AF.Abs), reads=["f4"], writes=["t1"])
            P.op("act", lambda e: e.activation(out=t1[:], in_=t1[:], func=AF.Exp, scale=-1.0), reads=["t1"], writes=["t1"])
            P.op("act", lambda e: e.activation(out=t1[:], in_=t1[:], func=AF.Ln, bias=C.onec[0:4, 0:1], scale=1.0), reads=["t1", "onec"], writes=["t1"])
            P.op("dve", lambda e: e.tensor_single_scalar(out=t2[:], in_=f4[:], scalar=0.0, op=ALU.min), reads=["f4"], writes=["t2"])
            P.op("dve", lambda e: e.tensor_tensor(out=t2[:], in0=t2[:], in1=t1[:], op=ALU.subtract), reads=["t1", "t2"], writes=["t2"])
            for qc in range(NQC):
                sl = slice(qc * 512, (qc + 1) * 512)
                init = 0.0 if qc == 0 else cc[:, qc * 512 - 1:qc * 512]
                P.op("dve", lambda e, sl=sl, init=init: e.tensor_tensor_scan(out=cc[:, sl], data0=one4[:], data1=t2[:, sl], initial=init,
                                                                               op0=ALU.mult, op1=ALU.add), reads=["t2", "one4", "cc"], writes=["cc"])
            P.op("dve", lambda e: e.tensor_single_scalar(out=t1[:], in_=cc[:], scalar=-1.0, op=ALU.mult), reads=["cc"], writes=["t1"])
            P.op("dve", lambda e: e.tensor_copy(out=hb[0][:], in_=t1[:]), reads=["t1"], writes=["hb0"])
            P.op("dve", lambda e: e.tensor_tensor(out=t1[:], in0=t1[:], in1=hb[0][:], op=ALU.subtract), reads=["t1", "hb0"], writes=["t1"])
            P.op("dve", lambda e: e.tensor_copy(out=hb[1][:], in_=t1[:]), reads=["t1"], writes=["hb1"])
            P.op("dve", lambda e: e.tensor_tensor(out=t1[:], in0=t1[:], in1=hb[1][:], op=ALU.subtract), reads=["t1", "hb1"], writes=["t1"])
            P.op("dve", lambda e: e.tensor_copy(out=hb[2][:], in_=t1[:]), reads=["t1"], writes=["hb2"])
            P.op("dve", lambda e: e.tensor_single_scalar(out=hb[3][:], in_=hb[0][:], scalar=-1.0, op=ALU.mult), reads=["hb0"], writes=["hb3"])
            for h in range(4):
                for lv in range(3):
                    P.dma("sp", lambda e, h=h, lv=lv: e.dma_start(out=KF[64 + lv:65 + lv, h, :], in_=hb[lv][h:h + 1, :]),
                          reads=["hb%d" % lv], writes=["KF"], pool="l", npool=6)
                P.dma("sp", lambda e, h=h: e.dma_start(out=QF[67:68, h, :], in_=hb[3][h:h + 1, :]), reads=["hb3"], writes=["QF"], pool="l", npool=6)
            P.barrier()
        with contextlib.ExitStack() as s2:
            GT = [P.sb("GT%d" % i, [64, S], BF16, s2) for i in range(2)]
            OS = [P.sb("OS%d" % i, [64, S], BF16, s2) for i in range(2)]
            rz = P.sb("rz", [64, 512], F32, s2)
            tmp = P.sb("tmp", [64, 512], F32, s2)
            for h in range(4):
                g, r0 = h // 2, (h % 2) * 64
                gt, os_ = GT[h % 2], OS[h % 2]
                gk, ok_ = "GT%d" % (h % 2), "OS%d" % (h % 2)
                P.dma("pool", lambda e, gt=gt, g=g, r0=r0: e.dma_start(out=gt[:], in_=PFb[PFB_ROW["fg%d" % g] + r0:PFB_ROW["fg%d" % g] + r0 + 64, :]),
                      reads=["PFb"], writes=[gk], pool="l2", npool=6)
                for qc in range(NQC):
                    qs = slice(qc * 512, (qc + 1) * 512)
                    tiles = []
                    for kb in range(4 * qc + 4):
                        t = dict(lhsT=KF[:, h, kb * 128:(kb + 1) * 128], rhs=QF[:, h, qs], rk=["KF", "QF"],
                                 v=VA[:, kb, h, :], vk=["VA"], extra=None)
                        if kb >= 4 * qc:
                            off = qc * 512 - kb * 128
                            t["extra"] = (C.Jb[:], C.WRc[:, off + 384:off + 384 + 512], ["Jb", "WRc"])
                        tiles.append(t)
                    ob = C.obank[C.obi % 2]
                    obk = "obank%d" % (C.obi % 2)
                    C.obi += 1
                    attn_tiles(P, C, tiles, ob[:], obk, 512)
                    P.op("dve", lambda e, ob=ob: e.reciprocal(out=rz[:], in_=ob[64:128, :]), reads=[obk], writes=["rz"])
                    P.op("dve", lambda e, ob=ob: e.tensor_tensor(out=tmp[:], in0=ob[0:64, :], in1=rz[:], op=ALU.mult), reads=[obk, "rz"], writes=["tmp"])
                    P.op("pool", lambda e, os_=os_, gt=gt, qs=qs: e.tensor_tensor(out=os_[:, qs], in0=tmp[:], in1=gt[:, qs], op=ALU.mult),
                         reads=["tmp", gk], writes=[ok_])
                P.dma("sp", lambda e, os_=os_, h=h: e.dma_start(out=OT[h * 64:(h + 1) * 64, :], in_=os_[:]), reads=[ok_], writes=["OT"], pool="o", npool=4)
    P.barrier()


def phase_mem(P, C, S, PFb, memT, wkv, mg_, OT):
    NQC = S // 512
    with contextlib.ExitStack() as st:
        mx = P.sb("mx", [128, 8, 256], F32, st)
        msq = P.sb("msq", [128, 8, 256], BF16, st)
        mh = P.sb("mh", [128, 8, 256], BF16, st)
        mrt = P.sb("mrt", [128, 256], F32, st)
        mgc = P.sb("mgc", [128, 8], F32, st)
        wks = P.sb("wks", [128, 8, 512], F32, st)
        wkb = P.sb("wkb", [128, 8, 512], BF16, st)
        KM = P.sb("KM", [128, 2, 256], BF16, st)
        VM = P.sb("VM", [128, 2, 2, 128], BF16, st)
        QM = P.sb("QM", [128, 2, S], BF16, st)
        GM = P.sb("GM", [128, 2, S], BF16, st)
        OM = P.sb("OM", [128, 2, S], BF16, st)
        rz = P.sb("rzm", [128, 512], F32, st)
        tmp = P.sb("tmpm", [128, 512], F32, st)
        P.dma("sp", lambda e: e.dma_start(out=mx[:], in_=memT.rearrange("(k p) t -> p k t", p=128)), writes=["mx"], pool="l", npool=6)
        P.dma("sp", lambda e: e.dma_start(out=mgc[:], in_=mg_), writes=["mgc"], pool="l", npool=6)
        P.dma("pool", lambda e: e.dma_start(out=wks[:], in_=wkv.rearrange("(k p) c -> p k c", p=128)), writes=["wks"], pool="l2", npool=6)
        for g in range(2):
            P.dma("sp", lambda e, g=g: e.dma_start(out=QM[:, g, :], in_=PFb[PFB_ROW["mq%d" % g]:PFB_ROW["mq%d" % g] + 128, :]),
                  reads=["PFb"], writes=["QM"], pool="l", npool=6)
            P.dma("pool", lambda e, g=g: e.dma_start(out=GM[:, g, :], in_=PFb[PFB_ROW["mg%d" % g]:PFB_ROW["mg%d" % g] + 128, :]),
                  reads=["PFb"], writes=["GM"], pool="l2", npool=6)
        P.op("pool", lambda e: e.tensor_copy(out=wkb[:], in_=wks[:]), reads=["wks"], writes=["wkb"])
        P.op("act", lambda e: e.activation(out=msq[:], in_=mx[:], func=AF.Square), reads=["mx"], writes=["msq"])
        ps = C.psb[0]
        for k in range(8):
            mm(P, ps[:, 0:256], C.onesb[:, 0:128], msq[:, k, :], k == 0, k == 7, ["msq", "onesb"], ["psb0"])
        P.op("act", lambda e, ps=ps: e.activation(out=mrt[:], in_=ps[:, 0:256], func=AF.Sqrt, bias=C.epsc[:, 0:1], scale=1.0 / D_MODEL),
             reads=["psb0", "epsc"], writes=["mrt"])
        P.op("dve", lambda e: e.reciprocal(out=mrt[:], in_=mrt[:]), reads=["mrt"], writes=["mrt"])
        for k in range(8):
            P.op("dve", lambda e, k=k: e.scalar_tensor_tensor(out=mh[:, k, :], in0=mx[:, k, :], scalar=mgc[:, k:k + 1], in1=mrt[:],
                                                              op0=ALU.mult, op1=ALU.mult), reads=["mx", "mrt", "mgc"], writes=["mh"])
        for h in range(2):
            ps = C.psb[h % 2]
            pk = "psb%d" % (h % 2)
            for k in range(8):
                mm(P, ps[:, 0:256], wkb[:, k, h * 128:(h + 1) * 128], mh[:, k, :], k == 0, k == 7, ["wkb", "mh"], [pk])
            P.op("act", lambda e, h=h, ps=ps: e.activation(out=KM[:, h, :], in_=ps[:, 0:256], func=AF.Copy), reads=[pk], writes=["KM"])
        for mc in range(2):
            ps = C.psb[mc % 2]
            pk = "psb%d" % (mc % 2)
            for k in range(8):
                mm(P, ps[:, 0:256], mh[:, k, mc * 128:(mc + 1) * 128], wkb[:, k, 256:512], k == 0, k == 7, ["wkb", "mh"], [pk])
            P.op("act", lambda e, mc=mc, ps=ps: e.activation(out=VM[:, mc, :, :], in_=ps[:, 0:256].rearrange("p (h d) -> p h d", d=128), func=AF.Copy),
                 reads=[pk], writes=["VM"])
        for h in range(2):
            for qc in range(NQC):
                qs = slice(qc * 512, (qc + 1) * 512)
                sp = C.spair[C.spi % 2]
                spk = "spair%d" % (C.spi % 2)
                pt = C.ptile[C.spi % 2]
                ptk = "ptile%d" % (C.spi % 2)
                C.spi += 1
                for mc in range(2):
                    mm(P, sp[:, mc * 512:(mc + 1) * 512], KM[:, h, mc * 128:(mc + 1) * 128], QM[:, h, qs], True, True, ["KM", "QM"], [spk])
                P.op("act", lambda e, sp=sp, pt=pt: e.activation(out=pt[:], in_=sp[:], func=AF.Exp), reads=[spk], writes=[ptk])
                ob = C.obank[0]
                zb = C.obank[1]
                for mc in range(2):
                    mm(P, ob[:], VM[:, mc, h, :], pt[:, mc * 512:(mc + 1) * 512], mc == 0, mc == 1, [ptk, "VM"], ["obank0"])
                for mc in range(2):
                    mm(P, zb[:], C.onesb[:, 0:128], pt[:, mc * 512:(mc + 1) * 512], mc == 0, mc == 1, [ptk, "onesb"], ["obank1"])
                P.op("dve", lambda e: e.reciprocal(out=rz[:], in_=zb[:]), reads=["obank1"], writes=["rzm"])
                P.op("dve", lambda e: e.tensor_tensor(out=tmp[:], in0=ob[:], in1=rz[:], op=ALU.mult), reads=["obank0", "rzm"], writes=["tmpm"])
                P.op("pool", lambda e, h=h, qs=qs: e.tensor_tensor(out=OM[:, h, qs], in0=tmp[:], in1=GM[:, h, qs], op=ALU.mult),
                     reads=["tmpm", "GM"], writes=["OM"])
        P.dma("sp", lambda e: e.dma_start(out=OT[768:1024, :].rearrange("(h d) t -> d h t", d=128), in_=OM[:]), reads=["OM"], writes=["OT"], pool="o", npool=4)
    P.barrier()


def make_consts():
    J = np.zeros((128, 128), np.float32)
    J[np.arange(128), 127 - np.arange(128)] = 1.0
    k = np.arange(128)[:, None]
    j = np.arange(896)[None, :]
    W = np.where(j - 384 - k >= 0, 0.0, NEG).astype(np.float32)
    WRc = W[::-1, :].copy()
    return {"cJ": J.astype(ml_dtypes.bfloat16), "cWRc": WRc.astype(ml_dtypes.bfloat16)}


def setup_ctx(P, nc, cJ, cWRc):
    C = Ctx()
    C.onesb = P.sb("onesb", [128, 128], BF16)
    C.epsc = P.sb("epsc", [128, 1], F32)
    C.onec = P.sb("onec", [128, 1], F32)
    C.Jb = P.sb("Jb", [128, 128], BF16)
    C.WRc = P.sb("WRc", [128, 896], BF16)
    C.psb = [P.ps("psb%d" % i, [128, 512]) for i in range(2)]
    C.spair = [P.ps("spair%d" % i, [128, 1024]) for i in range(2)]
    C.obank = [P.ps("obank%d" % i, [128, 512]) for i in range(2)]
    C.ptile = [P.sb("ptile%d" % i, [128, 1024], BF16) for i in range(2)]
    C.spi = 0
    C.obi = 0
    P.op("pool", lambda e: e.memset(C.onesb[:], 1.0), writes=["onesb"])
    P.op("pool", lambda e: e.memset(C.epsc[:], EPS), writes=["epsc"])
    P.op("pool", lambda e: e.memset(C.onec[:], 1.0), writes=["onec"])
    P.dma("sp", lambda e: e.dma_start(out=C.Jb[:], in_=cJ), writes=["Jb"])
    P.dma("sp", lambda e: e.dma_start(out=C.WRc[:], in_=cWRc), writes=["WRc"])
    return C


def bc(ap, shape):
    return ap.broadcast_to(list(shape))


def phase_ssd(P, C, S, PFb, PFf, cw, cb, dtb, alog, dsk, ngs, OT):
    NB = S // 128
    NQC = S // 512
    with contextlib.ExitStack() as st:
        XC = P.sb("XC", [128, 4, S], BF16, st)
        XT = P.sb("XT", [128, NB, 384], BF16, st)
        SZ = P.sb("SZ", [128, 2, S], BF16, st)
        cwt = P.sb("cwt", [128, 4, 4], F32, st)
        cbt = P.sb("cbt", [128, 4], F32, st)
        dtbt = P.sb("dtbt", [4, 1], F32, st)
        alt = P.sb("alt", [4, 1], F32, st)
        dskt = P.sb("dskt", [128, 256], F32, st)
        ngt = P.sb("ngt", [128, 2], F32, st)
        dtT = P.sb("dtT", [128, NB, 4], F32, st)
        aT = P.sb("aT", [128, NB, 4], F32, st)
        el = P.sb("el", [128, NB, 4], F32, st)
        dec = P.sb("dec", [128, NB, 4], F32, st)
        dtw = P.sb("dtw", [128, NB, 4], F32, st)
        acs = P.sb("acs", [128, NB, 4], F32, st)
        for nm, t, src in (("cwt", cwt, cw), ("cbt", cbt, cb), ("dtbt", dtbt, dtb), ("alt", alt, alog), ("dskt", dskt, dsk), ("ngt", ngt, ngs)):
            P.dma("sp", lambda e, t=t, src=src: e.dma_start(out=t[:], in_=src), writes=[nm], pool="l", npool=6)
        for g in range(2):
            P.dma("pool", lambda e, g=g: e.dma_start(out=SZ[:, g, :], in_=PFb[PFB_ROW["sz%d" % g]:PFB_ROW["sz%d" % g] + 128, :]),
                  reads=["PFb"], writes=["SZ"], pool="l2", npool=6)
        with contextlib.ExitStack() as sa:
            XP = [P.sb("XP%d" % i, [128, S + 3], F32, sa) for i in range(2)]
            acc = P.sb("acc", [128, S], F32, sa)
            dr = P.sb("dr", [4, S], F32, sa)
            d1 = P.sb("d1", [4, S], F32, sa)
            d2 = P.sb("d2", [4, S], F32, sa)
            for i in range(2):
                P.op("pool", lambda e, i=i: e.memset(XP[i][:, 0:3], 0.0), writes=["XP%d" % i])
            names = ["sx0", "sx1", "sB", "sC"]
            for g in range(4):
                xp = XP[g % 2]
                xk = "XP%d" % (g % 2)
                r0 = PFF_ROW[names[g]]
                P.dma("sp", lambda e, xp=xp, r0=r0: e.dma_start(out=xp[:, 3:3 + S], in_=PFf[r0:r0 + 128, :]), reads=["PFf"], writes=[xk], pool="l", npool=6)
                eng = "dve"
                P.op(eng, lambda e, xp=xp, g=g: e.tensor_scalar(out=acc[:], in0=xp[:, 0:S], scalar1=cwt[:, g, 0:1], scalar2=None, op0=ALU.mult),
                     reads=[xk, "cwt"], writes=["acc"])
                for k in range(1, 4):
                    P.op(eng, lambda e, xp=xp, g=g, k=k: e.scalar_tensor_tensor(out=acc[:], in0=xp[:, k:k + S], scalar=cwt[:, g, k:k + 1], in1=acc[:],
                                                                                op0=ALU.mult, op1=ALU.add), reads=[xk, "cwt", "acc"], writes=["acc"])
                P.op("act", lambda e, g=g: e.activation(out=XC[:, g, :], in_=acc[:], func=AF.Silu, bias=cbt[:, g:g + 1], scale=1.0),
                     reads=["acc", "cbt"], writes=["XC"])
            P.dma("sp", lambda e: e.dma_start(out=dr[:], in_=PFf[PFF_ROW["small"] + 4:PFF_ROW["small"] + 8, :]), reads=["PFf"], writes=["dr"], pool="l", npool=6)
            P.op("dve", lambda e: e.tensor_scalar(out=dr[:], in0=dr[:], scalar1=dtbt[:, 0:1], scalar2=None, op0=ALU.add), reads=["dr", "dtbt"], writes=["dr"])
            P.op("act", lambda e: e.activation(out=d1[:], in_=dr[:], func=AF.Abs), reads=["dr"], writes=["d1"])
            P.op("act", lambda e: e.activation(out=d1[:], in_=d1[:], func=AF.Exp, scale=-1.0), reads=["d1"], writes=["d1"])
            P.op("act", lambda e: e.activation(out=d1[:], in_=d1[:], func=AF.Ln, bias=C.onec[0:4, 0:1], scale=1.0), reads=["d1", "onec"], writes=["d1"])
            P.op("dve", lambda e: e.tensor_single_scalar(out=d2[:], in_=dr[:], scalar=0.0, op=ALU.max), reads=["dr"], writes=["d2"])
            P.op("dve", lambda e: e.tensor_tensor(out=d1[:], in0=d1[:], in1=d2[:], op=ALU.add), reads=["d1", "d2"], writes=["d1"])
            P.op("act", lambda e: e.activation(out=alt[:], in_=alt[:], func=AF.Exp), reads=["alt"], writes=["alt"])
            P.op("dve", lambda e: e.tensor_scalar(out=d2[:], in0=d1[:], scalar1=alt[:, 0:1], scalar2=-1.0, op0=ALU.mult, op1=ALU.mult),
                 reads=["d1", "alt"], writes=["d2"])
            for c in range(NB):
                P.op("pe", lambda e, c=c: e.transpose(out=C.psb[0][:, c * 4:(c + 1) * 4], in_=d1[0:4, c * 128:(c + 1) * 128], identity=C.identf[0:4, 0:4]),
                     reads=["d1", "identf"], writes=["psb0"])
                P.op("pe", lambda e, c=c: e.transpose(out=C.psb[1][:, c * 4:(c + 1) * 4], in_=d2[0:4, c * 128:(c + 1) * 128], identity=C.identf[0:4, 0:4]),
                     reads=["d2", "identf"], writes=["psb1"])
            P.op("dve", lambda e: e.tensor_copy(out=dtT[:].rearrange("p c h -> p (c h)"), in_=C.psb[0][:, 0:NB * 4]), reads=["psb0"], writes=["dtT"])
            P.op("dve", lambda e: e.tensor_copy(out=aT[:].rearrange("p c h -> p (c h)"), in_=C.psb[1][:, 0:NB * 4]), reads=["psb1"], writes=["aT"])
            aflat = aT[:].rearrange("p c h -> p (c h)")
            mm(P, C.psb[0][:, 0:NB * 4], C.Umat[:], aflat, True, True, ["Umat", "aT"], ["psb0"])
            mm(P, C.psb[1][:, 0:NB * 4], C.onesf[:], aflat, True, True, ["onesf", "aT"], ["psb1"])
            fl = lambda t: t[:].rearrange("p c h -> p (c h)")
            P.op("act", lambda e: e.activation(out=fl(el), in_=C.psb[0][:, 0:NB * 4], func=AF.Exp), reads=["psb0"], writes=["el"])
            P.op("act", lambda e: e.activation(out=fl(dec), in_=C.psb[1][:, 0:NB * 4], func=AF.Exp), reads=["psb1"], writes=["dec"])
            P.op("act", lambda e: e.activation(out=fl(acs), in_=C.psb[0][:, 0:NB * 4], func=AF.Copy), reads=["psb0"], writes=["acs"])
            P.op("dve", lambda e: e.tensor_tensor(out=fl(dtw), in0=C.psb[1][:, 0:NB * 4], in1=fl(acs), op=ALU.subtract), reads=["psb1", "acs"], writes=["dtw"])
            P.op("act", lambda e: e.activation(out=fl(dtw), in_=fl(dtw), func=AF.Exp), reads=["dtw"], writes=["dtw"])
            P.op("dve", lambda e: e.tensor_tensor(out=fl(dtw), in0=fl(dtw), in1=fl(dtT), op=ALU.mult), reads=["dtw", "dtT"], writes=["dtw"])
            for c in range(NB):
                pv = C.psb[c % 2][:].bitcast(BF16)
                pk = "psb%d" % (c % 2)
                for j, g in enumerate((0, 1, 2)):
                    P.op("pe", lambda e, pv=pv, j=j, g=g, c=c: e.transpose(out=pv[:, j * 128:(j + 1) * 128], in_=XC[:, g, c * 128:(c + 1) * 128], identity=C.identb[:]),
                         reads=["XC", "identb"], writes=[pk])
                if c % 2 == 0:
                    P.op("act", lambda e, pv=pv, c=c: e.activation(out=XT[:, c, :], in_=pv[:, 0:384], func=AF.Copy), reads=[pk], writes=["XT"])
                else:
                    P.op("dve", lambda e, pv=pv, c=c: e.tensor_copy(out=XT[:, c, :], in_=pv[:, 0:384]), reads=[pk], writes=["XT"])
        P.barrier()
        with contextlib.ExitStack() as sb_:
            Y = P.sb("Y", [128, NB, 256], F32, sb_)
            R1 = P.sb("R1", [128, 4, 128], F32, sb_)
            R2 = P.sb("R2", [128, 4, 128], F32, sb_)
            ED = P.sb("ED", [128, 4, 128], F32, sb_)
            MT = P.sb("MT", [128, 4, 128], BF16, sb_)
            xdt = P.sb("xdt", [128, 4, 64], BF16, sb_)
            xdw = P.sb("xdw", [128, 4, 64], BF16, sb_)
            st32 = P.sb("st32", [128, 4, 64], F32, sb_)
            stb_ = P.sb("stbf", [128, 4, 64], BF16, sb_)
            tq = P.sb("tq", [128, 4, 64], F32, sb_)
            t2 = P.sb("t2s", [128, 256], F32, sb_)
            P.op("pool", lambda e: e.memset(st32[:], 0.0), writes=["st32"])
            P.op("pool", lambda e: e.memset(stb_[:], 0.0), writes=["stbf"])
            for c in range(NB):
                cs_ = slice(c * 128, (c + 1) * 128)
                a_bc = bc(aT[:, c, :].unsqueeze(2), [128, 4, 128])
                P.op("dve", lambda e, a_bc=a_bc: e.tensor_tensor(out=R1[:], in0=bc(C.Umat[:].unsqueeze(1), [128, 4, 128]), in1=a_bc, op=ALU.mult),
                     reads=["Umat", "aT"], writes=["R1"])
                P.op("pool", lambda e, a_bc=a_bc: e.tensor_tensor(out=R2[:], in0=bc(C.SHm[:].unsqueeze(1), [128, 4, 128]), in1=a_bc, op=ALU.subtract),
                     reads=["SHm", "aT"], writes=["R2"])
                sp = C.spair[c % 2]
                spk = "spair%d" % (c % 2)
                mm(P, sp[:, 0:512], C.onesf[:], R1[:].rearrange("p h l -> p (h l)"), True, False, ["onesf", "R1"], [spk])
                mm(P, sp[:, 0:512], C.Umat[:], R2[:].rearrange("p h l -> p (h l)"), False, True, ["Umat", "R2"], [spk])
                P.op("act", lambda e, sp=sp: e.activation(out=ED[:].rearrange("p h l -> p (h l)"), in_=sp[:, 0:512], func=AF.Exp), reads=[spk], writes=["ED"])
                pg = C.psb[c % 2]
                pgk = "psb%d" % (c % 2)
                mm(P, pg[:, 0:128], XC[:, 2, cs_], XC[:, 3, cs_], True, True, ["XC"], [pgk])
                P.op("dve", lambda e, pg=pg: e.tensor_tensor(out=MT[:], in0=ED[:], in1=bc(pg[:, 0:128].unsqueeze(1), [128, 4, 128]), op=ALU.mult),
                     reads=["ED", pgk], writes=["MT"])
                xv = XT[:, c, 0:256].rearrange("p (h d) -> p h d", d=64)
                P.op("pool", lambda e, xv=xv, c=c: e.tensor_tensor(out=xdt[:], in0=xv, in1=bc(dtT[:, c, :].unsqueeze(2), [128, 4, 64]), op=ALU.mult),
                     reads=["XT", "dtT"], writes=["xdt"])
                P.op("pool", lambda e, xv=xv, c=c: e.tensor_tensor(out=xdw[:], in0=xv, in1=bc(dtw[:, c, :].unsqueeze(2), [128, 4, 64]), op=ALU.mult),
                     reads=["XT", "dtw"], writes=["xdw"])
                py = C.obank[0]
                po = C.obank[1]
                for h in range(4):
                    mm(P, py[:, h * 64:(h + 1) * 64], MT[:, h, :], xdt[:, h, :], True, True, ["MT", "xdt"], ["obank0"])
                mm(P, po[:, 0:256], XC[:, 3, cs_], stb_[:].rearrange("p h d -> p (h d)"), True, True, ["XC", "stbf"], ["obank1"])
                pc = C.psb[(c + 1) % 2]
                pck = "psb%d" % ((c + 1) % 2)
                mm(P, pc[:, 256:512], XT[:, c, 256:384], xdw[:].rearrange("p h d -> p (h d)"), True, True, ["XT", "xdw"], [pck])
                P.op("dve", lambda e, c=c: e.tensor_tensor(out=tq[:], in0=po[:, 0:256].rearrange("p (h d) -> p h d", d=64),
                                                          in1=bc(el[:, c, :].unsqueeze(2), [128, 4, 64]), op=ALU.mult), reads=["obank1", "el"], writes=["tq"])
                P.op("pool", lambda e, c=c: e.tensor_tensor(out=t2[:], in0=XT[:, c, 0:256], in1=dskt[:], op=ALU.mult), reads=["XT", "dskt"], writes=["t2s"])
                P.op("dve", lambda e, c=c: e.tensor_tensor(out=Y[:, c, :], in0=py[:, 0:256], in1=tq[:].rearrange("p h d -> p (h d)"), op=ALU.add),
                     reads=["obank0", "tq"], writes=["Y"])
                P.op("pool", lambda e, c=c: e.tensor_tensor(out=Y[:, c, :], in0=Y[:, c, :], in1=t2[:], op=ALU.add), reads=["Y", "t2s"], writes=["Y"])
                P.op("pool", lambda e, c=c: e.tensor_tensor(out=st32[:], in0=st32[:], in1=bc(dec[:, c, :].unsqueeze(2), [128, 4, 64]), op=ALU.mult),
                     reads=["st32", "dec", "stbf"], writes=["st32"])
                P.op("dve", lambda e, pc=pc: e.tensor_tensor(out=st32[:], in0=pc[:, 256:512].rearrange("p (h d) -> p h d", d=64), in1=st32[:], op=ALU.add),
                     reads=[pck, "st32"], writes=["st32"])
                P.op("act", lambda e: e.activation(out=stb_[:], in_=st32[:], func=AF.Copy), reads=["st32"], writes=["stbf"])
            with contextlib.ExitStack() as sc:
                YG = P.sb("YG", [128, 2, S], F32, sc)
                sqy = P.sb("sqy", [128, 2, 512], BF16, sc)
                rty = P.sb("rty", [128, 512], F32, sc)
                OSs = P.sb("OSs", [128, 2, S], BF16, sc)
                for c in range(NB):
                    pt_ = C.spair[c % 2]
                    ptk = "spair%d" % (c % 2)
                    for g in range(2):
                        P.op("pe", lambda e, pt_=pt_, g=g, c=c: e.transpose(out=pt_[:, g * 128:(g + 1) * 128], in_=Y[:, c, g * 128:(g + 1) * 128], identity=C.identf[:]),
                             reads=["Y", "identf"], writes=[ptk])
                    P.op("dve", lambda e, pt_=pt_, c=c: e.tensor_tensor(out=YG[:, :, c * 128:(c + 1) * 128], in0=pt_[:, 0:256].rearrange("p (g t) -> p g t", t=128),
                                                                     in1=SZ[:, :, c * 128:(c + 1) * 128], op=ALU.mult), reads=[ptk, "SZ"], writes=["YG"])
                for qc in range(NQC):
                    qs = slice(qc * 512, (qc + 1) * 512)
                    P.op("act", lambda e, qs=qs: e.activation(out=sqy[:], in_=YG[:, :, qs], func=AF.Square), reads=["YG"], writes=["sqy"])
                    ps = C.psb[qc % 2]
                    pk = "psb%d" % (qc % 2)
                    for g in range(2):
                        mm(P, ps[:], C.onesb[:], sqy[:, g, :], g == 0, g == 1, ["sqy", "onesb"], [pk])
                    P.op("act", lambda e, ps=ps: e.activation(out=rty[:], in_=ps[:], func=AF.Sqrt, bias=C.epsc[:, 0:1], scale=1.0 / 256), reads=[pk, "epsc"], writes=["rty"])
                    P.op("dve", lambda e: e.reciprocal(out=rty[:], in_=rty[:]), reads=["rty"], writes=["rty"])
                    for g in range(2):
                        P.op("dve", lambda e, g=g, qs=qs: e.scalar_tensor_tensor(out=OSs[:, g, qs], in0=YG[:, g, qs], scalar=ngt[:, g:g + 1], in1=rty[:],
                                                                                              op0=ALU.mult, op1=ALU.mult), reads=["YG", "ngt", "rty"], writes=["OSs"])
                P.dma("sp", lambda e: e.dma_start(out=OT[256:512, :].rearrange("(g p) t -> p g t", p=128), in_=OSs[:]), reads=["OSs"], writes=["OT"], pool="o", npool=4)
                P.barrier()
    P.barrier()


TINY = 1e-30
GC_OFF = 4111


def nsa_host(S, table4, pe, w1, w2):
    nsel = S // 64
    NB = S // 128
    n_cmp = S // 16 - 1
    text = np.concatenate([table4, np.full((1, 4), NEG, np.float32)], 0)
    LC = S + 4080
    dC = np.arange(LC) - GC_OFF
    idxC = np.where(dC < 0, 32, t5_bucket(dC))
    dS = np.arange(383) - 127
    idxS = np.where(dS < 0, 32, t5_bucket(dS))
    dW = np.arange(767) - 127
    idxW = np.where((dW < 0) | (dW >= 512), 32, t5_bucket(dW))
    GALL = np.ascontiguousarray(np.concatenate([text[idxC], text[idxS], text[idxW]], 0).T.astype(np.float32))
    t31 = np.ascontiguousarray(table4[31].reshape(4, 1))
    E = np.zeros((64, S), np.float32)
    kk = np.arange(S)
    E[kk // 64, kk] = 1.0
    NCC = (n_cmp + 127) // 128
    cidx = np.arange(NCC * 128)
    cstart = cidx * 16
    ss = np.arange(nsel) * 64
    ov = ((cstart[:, None] < ss[None, :] + 64) & (cstart[:, None] + 32 > ss[None, :]) & (cidx[:, None] < n_cmp)).astype(np.float32)
    OVL = np.concatenate([ov, np.ones((NCC * 128, 1), np.float32)], 1).reshape(NCC, 128, nsel + 1).transpose(1, 0, 2)
    qpos = np.arange(S)
    cur = qpos // 64
    jj = np.arange(nsel)[None, :]
    forced = (jj == 0) | (jj == cur[:, None]) | (jj == cur[:, None] - 1)
    valid = jj <= cur[:, None]
    FA = (valid & ~forced).astype(np.float32)
    FB = np.where(valid, np.where(forced, 1e9, 0.0), -1e9).astype(np.float32)
    FA = FA.reshape(NB, 128, nsel).transpose(1, 0, 2)
    FB = FB.reshape(NB, 128, nsel).transpose(1, 0, 2)
    pet = np.concatenate([pe[0].T, pe[1].T], 0)
    w1t = np.concatenate([w1[0].reshape(32, 64, 128).transpose(1, 0, 2), w1[1].reshape(32, 64, 128).transpose(1, 0, 2)], 0)
    w2t = np.concatenate([w2[0], w2[1]], 1)
    return {"nGALL": GALL, "nt31": t31, "nE": E.astype(ml_dtypes.bfloat16), "nOVL": np.ascontiguousarray(OVL).astype(ml_dtypes.bfloat16),
            "nFA": np.ascontiguousarray(FA), "nFB": np.ascontiguousarray(FB), "npe": np.ascontiguousarray(pet.astype(np.float32)),
            "nw1": np.ascontiguousarray(w1t.astype(np.float32)), "nw2": np.ascontiguousarray(w2t.astype(np.float32))}


def phase_nsa(P, C, S, PFb, PFf, VT, A, GALLd, GLd, OT):
    NB = S // 128
    nsel = S // 64
    n_cmp = S // 16 - 1
    NCC = (n_cmp + 127) // 128
    LC = S + 4080
    LALL = LC + 383 + 767
    with contextlib.ExitStack() as st:
        QN = P.sb("QN", [128, 4, S], BF16, st)
        KS = P.sb("KS", [128, S], BF16, st)
        KW = P.sb("KW", [64, S], BF16, st)
        VS = P.sb("VS", [128, NB, 128], BF16, st)
        VW = P.sb("VW", [128, NB, 128], BF16, st)
        KCM = P.sb("KCM", [64, NCC * 128], BF16, st)
        VCM = P.sb("VCM", [128, NCC, 128], BF16, st)
        OVL = P.sb("OVL", [128, NCC, nsel + 1], BF16, st)
        FA = P.sb("FA", [128, NB, nsel], F32, st)
        FB = P.sb("FB", [128, NB, nsel], F32, st)
        WRs = P.sb("WRs", [128, 4, 256], BF16, st)
        WRw = P.sb("WRw", [128, 4, 640], BF16, st)
        P.op("pool", lambda e: e.memset(QN[64:128, :, :], 0.0), writes=["QNm"])
        for h in range(4):
            g, r0 = h // 2, (h % 2) * 64
            P.dma("sp", lambda e, h=h, g=g, r0=r0: e.dma_start(out=QN[0:64, h, :], in_=PFb[PFB_ROW["nq%d" % g] + r0:PFB_ROW["nq%d" % g] + r0 + 64, :]),
                  reads=["PFb"], writes=["QNq"], pool="l", npool=6)
        r_kk = PFB_ROW["nkk"]
        P.dma("sp", lambda e: e.dma_start(out=KS[0:64, :], in_=PFb[r_kk:r_kk + 64, :]), reads=["PFb"], writes=["KS"], pool="l", npool=6)
        P.dma("sp", lambda e: e.dma_start(out=KS[64:128, :], in_=A["nE"]), writes=["KS"], pool="l", npool=6)
        P.dma("sp", lambda e: e.dma_start(out=KW[:], in_=PFb[r_kk + 64:r_kk + 128, :]), reads=["PFb"], writes=["KW"], pool="l", npool=6)
        P.dma("pool", lambda e: e.dma_start(out=VS[:], in_=VT[:, 4, :].rearrange("(b p) c -> p b c", p=128)), reads=["VT"], writes=["VS"], pool="l2", npool=6)
        P.dma("pool", lambda e: e.dma_start(out=VW[:], in_=VT[:, 5, :].rearrange("(b p) c -> p b c", p=128)), reads=["VT"], writes=["VW"], pool="l2", npool=6)
        P.dma("pool", lambda e: e.dma_start(out=OVL[:], in_=A["nOVL"]), writes=["OVL"], pool="l2", npool=6)
        P.dma("pool", lambda e: e.dma_start(out=FA[:], in_=A["nFA"]), writes=["FA"], pool="l2", npool=6)
        P.dma("pool", lambda e: e.dma_start(out=FB[:], in_=A["nFB"]), writes=["FB"], pool="l2", npool=6)
        with contextlib.ExitStack() as s1:
            gl = P.sb("gl", [12, S], F32, s1)
            gall = P.sb("gall", [4, LALL], F32, s1)
            t31 = P.sb("t31", [4, 1], F32, s1)
            wrf = P.sb("wrf", [128, 4, 640], F32, s1)
            P.dma("sp", lambda e: e.dma_start(out=gl[:], in_=PFf[PFF_ROW["small"] + 8:PFF_ROW["small"] + 20, :]), reads=["PFf"], writes=["gl"], pool="l", npool=6)
            P.op("act", lambda e: e.activation(out=gl[:], in_=gl[:], func=AF.Sigmoid), reads=["gl"], writes=["gl"])
            P.dma("sp", lambda e: e.dma_start(out=GLd, in_=gl[:]), reads=["gl"], writes=["GLd"], pool="o", npool=4)
            P.dma("sp", lambda e: e.dma_start(out=gall[:], in_=A["nGALL"]), writes=["gall"], pool="l", npool=6)
            P.dma("sp", lambda e: e.dma_start(out=t31[:], in_=A["nt31"]), writes=["t31"], pool="l", npool=6)
            P.op("dve", lambda e: e.tensor_scalar(out=gall[:], in0=gall[:], scalar1=t31[:, 0:1], scalar2=None, op0=ALU.subtract), reads=["gall", "t31"], writes=["gall"])
            P.dma("sp", lambda e: e.dma_start(out=GALLd, in_=gall[:]), reads=["gall"], writes=["GALLd"], pool="o", npool=4)
            hk = lambda off, n: bass.AP(tensor=GALLd.tensor, offset=off, ap=[[1, 128], [LALL, 4], [1, n]])
            P.dma("sp", lambda e: e.dma_start(out=wrf[:, :, 0:256], in_=hk(LC, 256)), reads=["GALLd"], writes=["wrf"], pool="l", npool=6)
            P.op("dve", lambda e: e.tensor_copy(out=WRs[:], in_=wrf[:, :, 0:256]), reads=["wrf"], writes=["WRs"])
            P.dma("sp", lambda e: e.dma_start(out=wrf[:], in_=hk(LC + 383, 640)), reads=["GALLd"], writes=["wrf"], pool="l", npool=6)
            P.op("dve", lambda e: e.tensor_copy(out=WRw[:], in_=wrf[:]), reads=["wrf"], writes=["WRw"])
            P.barrier()
            s1.close()
            s1b = contextlib.ExitStack()
            KCV = P.sb("KCV", [128, S], BF16, s1b)
            KA = P.sb("KA", [128, S], BF16, s1b)
            KB = P.sb("KB", [128, S], BF16, s1b)
            pet = P.sb("pet", [128, 32], F32, s1b)
            w1s = P.sb("w1s", [128, 32, 128], F32, s1b)
            w1b = P.sb("w1b", [128, 32, 128], BF16, s1b)
            w2s = P.sb("w2s", [128, 128], F32, s1b)
            w2b = P.sb("w2b", [128, 128], BF16, s1b)
            HS = P.sb("HS", [128, 2, NCC * 128], BF16, s1b)
            P.dma("sp", lambda e: e.dma_start(out=KCV[:], in_=PFb[PFB_ROW["ncv"]:PFB_ROW["ncv"] + 128, :]), reads=["PFb"], writes=["KCV"], pool="l", npool=6)
            P.dma("sp", lambda e: e.dma_start(out=pet[:], in_=A["npe"]), writes=["pet"], pool="l", npool=6)
            P.dma("pool", lambda e: e.dma_start(out=w1s[:], in_=A["nw1"]), writes=["w1s"], pool="l2", npool=6)
            P.dma("pool", lambda e: e.dma_start(out=w2s[:], in_=A["nw2"]), writes=["w2s"], pool="l2", npool=6)
            P.op("pool", lambda e: e.tensor_copy(out=w1b[:], in_=w1s[:]), reads=["w1s"], writes=["w1b"])
            P.op("pool", lambda e: e.tensor_copy(out=w2b[:], in_=w2s[:]), reads=["w2s"], writes=["w2b"])
            kv3 = KCV[:].rearrange("p (a b) -> p a b", b=16)
            P.op("dve", lambda e: e.tensor_tensor(out=KA[:].rearrange("p (a b) -> p a b", b=16), in0=kv3, in1=bc(pet[:, 0:16].unsqueeze(1), [128, S // 16, 16]), op=ALU.add),
                 reads=["KCV", "pet"], writes=["KA"])
            P.op("pool", lambda e: e.tensor_tensor(out=KB[:].rearrange("p (a b) -> p a b", b=16), in0=kv3, in1=bc(pet[:, 16:32].unsqueeze(1), [128, S // 16, 16]), op=ALU.add),
                 reads=["KCV", "pet"], writes=["KB"])
            P.op("pool", lambda e: e.memset(KCM[:], 0.0), writes=["KCM"])
            P.op("pool", lambda e: e.memset(VCM[:], 1.0), writes=["VCM"])
            P.op("pool", lambda e: e.memset(HS[:], 0.0), writes=["HS"])
            for kv in range(2):
                pr = slice(kv * 64, kv * 64 + 64)
                for cc in range(NCC):
                    c0 = cc * 128
                    ncc = min(128, n_cmp - c0)
                    ps = C.psb[(kv * NCC + cc) % 2]
                    pk = "psb%d" % ((kv * NCC + cc) % 2)
                    for l in range(32):
                        src = KA if l < 16 else KB
                        st0 = 16 * c0 + l
                        rhs = src[pr, st0:st0 + 16 * (ncc - 1) + 1:16]
                        mm(P, ps[:, 0:ncc], w1b[pr, l, :], rhs, l == 0, l == 31, ["w1b", "KA", "KB"], [pk])
                    P.op("act", lambda e, ps=ps, kv=kv, c0=c0, ncc=ncc: e.activation(out=HS[:, kv, c0:c0 + ncc], in_=ps[:, 0:ncc], func=AF.Silu), reads=[pk], writes=["HS"])
            for cc in range(NCC):
                c0 = cc * 128
                ncc = min(128, n_cmp - c0)
                ps = C.psb[cc % 2]
                pk = "psb%d" % (cc % 2)
                mm(P, ps[0:64, 0:ncc], w2b[:, 0:64], HS[:, 0, c0:c0 + ncc], True, True, ["w2b", "HS"], [pk])
                P.op("act", lambda e, ps=ps, c0=c0, ncc=ncc: e.activation(out=KCM[:, c0:c0 + ncc], in_=ps[0:64, 0:ncc], func=AF.Copy), reads=[pk], writes=["KCM"])
                ps2 = C.obank[cc % 2]
                pk2 = "obank%d" % (cc % 2)
                mm(P, ps2[0:ncc, 0:64], HS[:, 1, c0:c0 + ncc], w2b[:, 64:128], True, True, ["w2b", "HS"], [pk2])
                P.op("dve", lambda e, ps2=ps2, cc=cc, ncc=ncc: e.tensor_copy(out=VCM[0:ncc, cc, 0:64], in_=ps2[0:ncc, 0:64]), reads=[pk2], writes=["VCM"])
            P.barrier()
            s1b.close()
        P.barrier()
        with contextlib.ExitStack() as s2:
            GB = P.sb("GB", [64, 12, 512], F32, s2)
            GTs = P.sb("GTs", [64, 4, 512], BF16, s2)
            OSn = [P.sb("OSn%d" % i, [64, 4, 512], BF16, s2) for i in range(2)]
            wcf = [P.sb("wcf%d" % i, [128, 4, 128], F32, s2) for i in range(2)]
            wcb = [P.sb("wcb%d" % i, [128, 4, 128], BF16, s2) for i in range(2)]
            rzU = P.sb("rzU", [128, 4], F32, s2)
            imp = P.sb("imp", [128, nsel], F32, s2)
            imp2 = P.sb("imp2", [128, nsel], F32, s2)
            m8a = P.sb("m8a", [128, 8], F32, s2)
            m8b = P.sb("m8b", [128, 8], F32, s2)
            zc = P.sb("zc", [64, 4, 128], F32, s2)
            tq = P.sb("tqn", [64, 4, 128], F32, s2)
            acc = P.sb("accn", [64, 4, 128], F32, s2)
            wci = 0
            for qb in range(NB):
                seg, qi = qb // 4, qb % 4
                qs = slice(qb * 128, (qb + 1) * 128)
                ls = slice(qi * 128, (qi + 1) * 128)
                osn = OSn[seg % 2]
                osk = "OSn%d" % (seg % 2)
                if qi == 0:
                    P.dma("sp", lambda e, seg=seg: e.dma_start(out=GB[:], in_=bass.AP(tensor=GLd.tensor, offset=seg * 512, ap=[[0, 64], [S, 12], [1, 512]])),
                          reads=["GLd"], writes=["GB"], pool="l", npool=6)
                    P.dma("pool", lambda e, seg=seg: e.dma_start(out=GTs[:], in_=PFb[PFB_ROW["ng0"]:PFB_ROW["ng0"] + 256, seg * 512:(seg + 1) * 512].rearrange("(h d) t -> d h t", d=64)),
                          reads=["PFb"], writes=["GTs"], pool="l2", npool=6)
                qrhs = QN[0:64, :, qs]
                tiles = []
                for cc in range(NCC):
                    c0 = cc * 128
                    if qb * 128 + 127 < 16 * c0 + 31:
                        continue
                    wf, wb = wcf[wci % 2], wcb[wci % 2]
                    wfk, wbk = "wcf%d" % (wci % 2), "wcb%d" % (wci % 2)
                    wci += 1
                    off = qb * 128 - 16 * c0 - 2063 + GC_OFF
                    P.dma("sp", lambda e, wf=wf, off=off: e.dma_start(out=wf[:], in_=bass.AP(tensor=GALLd.tensor, offset=off, ap=[[16, 128], [LALL, 4], [1, 128]])),
                          reads=["GALLd"], writes=[wfk], pool="h", npool=2)
                    P.op("pool", lambda e, wf=wf, wb=wb: e.tensor_copy(out=wb[:], in_=wf[:]), reads=[wfk], writes=[wbk])
                    tiles.append(dict(lhsT=KCM[:, c0:c0 + 128], rhs=qrhs, rk=["KCM", "QNq"], extra=(C.Jb[:], wb[:], ["Jb", wbk]), v=VCM[:, cc, :], vk=["VCM"], cc=cc))
                ob = C.obank[C.obi % 2]
                obk = "obank%d" % (C.obi % 2)
                C.obi += 1
                spi0 = C.spi
                attn_tiles(P, C, tiles, ob[:], obk, 512)
                pt = C.ptile[spi0 % 2]
                ptk = "ptile%d" % (spi0 % 2)
                pu = C.psb[qb % 2]
                puk = "psb%d" % (qb % 2)
                for h in range(4):
                    for j, t in enumerate(tiles):
                        mm(P, pu[:, h * 72:h * 72 + nsel + 1], pt[:, j * 512 + h * 128:j * 512 + (h + 1) * 128], OVL[:, t["cc"], :], j == 0, j == len(tiles) - 1,
                           [ptk, "OVL"], [puk])
                pu3 = pu[:, 0:288].rearrange("p (h c) -> p h c", c=72)
                P.op("dve", lambda e, pu3=pu3: e.tensor_scalar(out=rzU[:], in0=pu3[:, :, nsel], scalar1=TINY, scalar2=None, op0=ALU.max), reads=[puk], writes=["rzU"])
                P.op("dve", lambda e: e.reciprocal(out=rzU[:], in_=rzU[:]), reads=["rzU"], writes=["rzU"])
                P.op("dve", lambda e, pu3=pu3: e.tensor_scalar(out=imp[:], in0=pu3[:, 0, 0:nsel], scalar1=rzU[:, 0:1], scalar2=None, op0=ALU.mult), reads=[puk, "rzU"], writes=["imp"])
                for h in range(1, 4):
                    P.op("dve", lambda e, pu3=pu3, h=h: e.scalar_tensor_tensor(out=imp[:], in0=pu3[:, h, 0:nsel], scalar=rzU[:, h:h + 1], in1=imp[:], op0=ALU.mult, op1=ALU.add),
                         reads=[puk, "rzU", "imp"], writes=["imp"])
                P.op("dve", lambda e, qb=qb: e.tensor_tensor(out=imp[:], in0=imp[:], in1=FA[:, qb, :], op=ALU.mult), reads=["imp", "FA"], writes=["imp"])
                P.op("dve", lambda e, qb=qb: e.tensor_tensor(out=imp[:], in0=imp[:], in1=FB[:, qb, :], op=ALU.add), reads=["imp", "FB"], writes=["imp"])
                P.op("dve", lambda e: e.max(out=m8a[:], in_=imp[:]), reads=["imp"], writes=["m8a"])
                P.op("dve", lambda e: e.match_replace(out=imp2[:], in_to_replace=m8a[:], in_values=imp[:], imm_value=-3e9), reads=["imp", "m8a"], writes=["imp2"])
                P.op("dve", lambda e: e.max(out=m8b[:], in_=imp2[:]), reads=["imp2"], writes=["m8b"])
                P.op("dve", lambda e: e.tensor_scalar(out=imp2[:], in0=imp[:], scalar1=m8b[:, 7:8], scalar2=None, op0=ALU.is_ge), reads=["imp", "m8b"], writes=["imp2"])
                P.op("dve", lambda e: e.tensor_scalar(out=imp2[:], in0=imp2[:], scalar1=1.0, scalar2=-NEG, op0=ALU.subtract, op1=ALU.mult), reads=["imp2"], writes=["imp2"])
                pm = C.psb[(qb + 1) % 2]
                pmk = "psb%d" % ((qb + 1) % 2)
                P.op("pe", lambda e, pm=pm: e.transpose(out=pm[0:nsel, 0:128], in_=imp2[:], identity=C.identf[:]), reads=["imp2", "identf"], writes=[pmk])
                P.op("act", lambda e, pm=pm, qs=qs: e.activation(out=QN[64:64 + nsel, :, qs], in_=bc(pm[0:nsel, 0:128].unsqueeze(1), [nsel, 4, 128]), func=AF.Copy),
                     reads=[pmk], writes=["QNm"])

                def epilogue(ob, obk, br, first, ls=ls):
                    o3 = ob[0:64, :].rearrange("p (h t) -> p h t", t=128)
                    z3 = ob[64:128, :].rearrange("p (h t) -> p h t", t=128)
                    P.op("dve", lambda e: e.tensor_scalar(out=zc[:], in0=z3, scalar1=TINY, scalar2=None, op0=ALU.max), reads=[obk], writes=["zc"])
                    P.op("dve", lambda e: e.reciprocal(out=zc[:], in_=zc[:]), reads=["zc"], writes=["zc"])
                    P.op("pool", lambda e: e.tensor_tensor(out=zc[:], in0=zc[:], in1=GB[:, br * 4:(br + 1) * 4, ls], op=ALU.mult), reads=["zc", "GB"], writes=["zc"])
                    if first:
                        P.op("dve", lambda e: e.tensor_tensor(out=acc[:], in0=o3, in1=zc[:], op=ALU.mult), reads=[obk, "zc"], writes=["accn"])
                    else:
                        P.op("dve", lambda e: e.tensor_tensor(out=tq[:], in0=o3, in1=zc[:], op=ALU.mult), reads=[obk, "zc"], writes=["tqn"])
                        P.op("pool", lambda e: e.tensor_tensor(out=acc[:], in0=acc[:], in1=tq[:], op=ALU.add), reads=["accn", "tqn"], writes=["accn"])

                epilogue(ob, obk, 0, True)
                tiles = []
                for kb in range(max(0, qb - 4), qb + 1):
                    t = dict(lhsT=KW[:, kb * 128:(kb + 1) * 128], rhs=qrhs, rk=["KW", "QNq"], extra=None, v=VW[:, kb, :], vk=["VW"])
                    off = (qb - kb) * 128
                    if off in (0, 128, 512):
                        t["extra"] = (C.Jb[:], WRw[:, :, off:off + 128], ["Jb", "WRw"])
                    tiles.append(t)
                ob = C.obank[C.obi % 2]
                obk = "obank%d" % (C.obi % 2)
                C.obi += 1
                attn_tiles(P, C, tiles, ob[:], obk, 512)
                epilogue(ob, obk, 2, False)
                tiles = []
                for kb in range(qb + 1):
                    t = dict(lhsT=KS[:, kb * 128:(kb + 1) * 128], rhs=QN[:, :, qs], rk=["KS", "QNq", "QNm"], extra=None, v=VS[:, kb, :], vk=["VS"])
                    if qb - kb <= 1:
                        off = (qb - kb) * 128
                        t["extra"] = (C.Jb[:], WRs[:, :, off:off + 128], ["Jb", "WRs"])
                    tiles.append(t)
                ob = C.obank[C.obi % 2]
                obk = "obank%d" % (C.obi % 2)
                C.obi += 1
                attn_tiles(P, C, tiles, ob[:], obk, 512)
                epilogue(ob, obk, 1, False)
                P.op("pool", lambda e, osn=osn, ls=ls: e.tensor_tensor(out=osn[:, :, ls], in0=acc[:], in1=GTs[:, :, ls], op=ALU.mult), reads=["accn", "GTs"], writes=[osk])
                if qi == 3:
                    P.dma("sp", lambda e, osn=osn, seg=seg: e.dma_start(out=OT[512:768, seg * 512:(seg + 1) * 512].rearrange("(h d) t -> d h t", d=64), in_=osn[:]),
                          reads=[osk], writes=["OT"], pool="o", npool=4)
        P.barrier()
    P.barrier()


def phase_outproj(P, C, St, xT, OTf, wo, xTo):
    NQC = St // 512
    with contextlib.ExitStack() as st:
        Wo = P.sb("Wo", [128, 16, 1024], BF16, st)
        wos = [P.sb("wos%d" % i, [128, 16, 128], F32, st) for i in range(2)]
        OA = [P.sb("OA%d" % i, [128, 16, 512], BF16, st) for i in range(2)]
        XA = [P.sb("XA%d" % i, [128, 8, 512], F32, st) for i in range(2)]
        wv = wo.rearrange("(k p) c -> p k c", p=128)
        for fg in range(8):
            ws = wos[fg % 2]
            wk = "wos%d" % (fg % 2)
            P.dma("sp" if fg % 2 == 0 else "pool", lambda e, ws=ws, fg=fg: e.dma_start(out=ws[:], in_=wv[:, :, fg * 128:(fg + 1) * 128]), writes=[wk], pool="w", npool=2)
            P.op("pool" if fg % 2 == 0 else "act",
                 (lambda e, ws=ws, fg=fg: e.tensor_copy(out=Wo[:, :, fg * 128:(fg + 1) * 128], in_=ws[:])) if fg % 2 == 0 else
                 (lambda e, ws=ws, fg=fg: e.activation(out=Wo[:, :, fg * 128:(fg + 1) * 128], in_=ws[:], func=AF.Copy)),
                 reads=[wk], writes=["Wo"])
        ov = OTf.rearrange("(k p) t -> p k t", p=128)
        xv = xT.rearrange("(k p) t -> p k t", p=128)
        xov = xTo.rearrange("(k p) t -> p k t", p=128)
        for qc in range(NQC):
            qs = slice(qc * 512, (qc + 1) * 512)
            oa, xa = OA[qc % 2], XA[qc % 2]
            ok, xk = "OA%d" % (qc % 2), "XA%d" % (qc % 2)
            P.dma("sp", lambda e, oa=oa, qs=qs: e.dma_start(out=oa[:], in_=ov[:, :, qs]), reads=["OTf"], writes=[ok], pool="x", npool=2)
            P.dma("pool", lambda e, xa=xa, qs=qs: e.dma_start(out=xa[:], in_=xv[:, :, qs]), reads=["xTin"], writes=[xk], pool="x2", npool=2)
            for fg in range(8):
                ps = C.psb[fg % 2]
                pk = "psb%d" % (fg % 2)
                for k in range(16):
                    mm(P, ps[:], Wo[:, k, fg * 128:(fg + 1) * 128], oa[:, k, :], k == 0, k == 15, ["Wo", ok], [pk])
                P.op("dve", lambda e, ps=ps, xa=xa, fg=fg: e.tensor_tensor(out=xa[:, fg, :], in0=ps[:], in1=xa[:, fg, :], op=ALU.add), reads=[pk, xk], writes=[xk])
            P.dma("sp", lambda e, xa=xa, qs=qs: e.dma_start(out=xov[:, :, qs], in_=xa[:]), reads=[xk], writes=["xTout"], pool="o", npool=4)
    P.barrier()


def phase_final(P, C, St, xT, fg_, outT):
    NQC = St // 512
    with contextlib.ExitStack() as st:
        xs = [P.sb("fx%d" % i, [128, 8, 512], F32, st) for i in range(2)]
        sq = P.sb("fsq", [128, 8, 512], BF16, st)
        rt = P.sb("frt", [128, 512], F32, st)
        gcol = P.sb("fgc", [128, 8], F32, st)
        P.dma("sp", lambda e: e.dma_start(out=gcol[:], in_=fg_), writes=["fgc"], pool="l", npool=6)
        xv = xT.rearrange("(k p) t -> p k t", p=128)
        ov = outT.rearrange("(k p) t -> p k t", p=128)
        for qc in range(NQC):
            qs = slice(qc * 512, (qc + 1) * 512)
            xb = xs[qc % 2]
            xk = "fx%d" % (qc % 2)
            P.dma("sp", lambda e, xb=xb, qs=qs: e.dma_start(out=xb[:], in_=xv[:, :, qs]), reads=["xTout"], writes=[xk], pool="x", npool=2)
            P.op("act", lambda e, xb=xb: e.activation(out=sq[:], in_=xb[:], func=AF.Square), reads=[xk], writes=["fsq"])
            ps = C.psb[qc % 2]
            pk = "psb%d" % (qc % 2)
            for k in range(8):
                mm(P, ps[:], C.onesb[:, 0:128], sq[:, k, :], k == 0, k == 7, ["fsq", "onesb"], [pk])
            P.op("act", lambda e, ps=ps: e.activation(out=rt[:], in_=ps[:], func=AF.Sqrt, bias=C.epsc[:, 0:1], scale=1.0 / D_MODEL), reads=[pk, "epsc"], writes=["frt"])
            P.op("dve", lambda e: e.reciprocal(out=rt[:], in_=rt[:]), reads=["frt"], writes=["frt"])
            for k in range(8):
                P.op("dve", lambda e, xb=xb, k=k: e.scalar_tensor_tensor(out=xb[:, k, :], in0=xb[:, k, :], scalar=gcol[:, k:k + 1], in1=rt[:], op0=ALU.mult, op1=ALU.mult),
                     reads=[xk, "frt", "fgc"], writes=[xk])
            P.dma("sp", lambda e, xb=xb, qs=qs: e.dma_start(out=ov[:, :, qs], in_=xb[:]), reads=[xk], writes=["outT"], pool="o", npool=4)
    P.barrier()


def make_consts2():
    c = make_consts()
    c["cIf"] = np.eye(128, dtype=np.float32)
    c["cIb"] = np.eye(128, dtype=np.float32).astype(ml_dtypes.bfloat16)
    j = np.arange(128)[:, None]
    l = np.arange(128)[None, :]
    c["cU"] = (j <= l).astype(np.float32)
    c["cSH"] = np.where(l == j - 1, NEG, 0.0).astype(np.float32)
    return c


CONST_SPECS = [("cJ", [128, 128], BF16), ("cWRc", [128, 896], BF16), ("cIf", [128, 128], F32), ("cIb", [128, 128], BF16),
               ("cU", [128, 128], F32), ("cSH", [128, 128], F32)]


def setup_ctx2(P, nc, cd):
    C = setup_ctx(P, nc, cd["cJ"], cd["cWRc"])
    C.identf = P.sb("identf", [128, 128], F32)
    C.identb = P.sb("identb", [128, 128], BF16)
    C.Umat = P.sb("Umat", [128, 128], F32)
    C.SHm = P.sb("SHm", [128, 128], F32)
    C.onesf = P.sb("onesf", [128, 128], F32)
    P.op("pool", lambda e: e.memset(C.onesf[:], 1.0), writes=["onesf"])
    for nm, t in (("cIf", C.identf), ("cIb", C.identb), ("cU", C.Umat), ("cSH", C.SHm)):
        P.dma("sp", lambda e, nm=nm, t=t: e.dma_start(out=t[:], in_=cd[nm]), writes=[nm])
    P.barrier()
    return C


def layer_specs(S, pfx):
    nsel = S // 64
    NB = S // 128
    NCC = (S // 16 - 1 + 127) // 128
    LALL = S + 4080 + 383 + 767
    sp = [("wl", [1024, 3220], F32), ("gl", [128, 8], F32), ("fb", [4, 1], F32), ("wkv", [1024, 512], F32), ("mg", [128, 8], F32),
          ("cw", [128, 4, 4], F32), ("cb", [128, 4], F32), ("dtb", [4, 1], F32), ("alog", [4, 1], F32), ("dsk", [128, 256], F32), ("ngs", [128, 2], F32),
          ("nGALL", [4, LALL], F32), ("nt31", [4, 1], F32), ("nE", [64, S], BF16), ("nOVL", [128, NCC, nsel + 1], BF16),
          ("nFA", [128, NB, nsel], F32), ("nFB", [128, NB, nsel], F32), ("npe", [128, 32], F32), ("nw1", [128, 32, 128], F32), ("nw2", [128, 128], F32)]
    return [(pfx + n, s, d) for n, s, d in sp]


def layer_host(inp, l, hf, S, pfx):
    cols = core_cols(hf)
    d = {}
    d["wl"] = np.ascontiguousarray(inp["w_in"][l][:, cols])
    d["gl"] = np.ascontiguousarray(inp["norm_g"][l].reshape(8, 128).T)
    d["fb"] = inp["fox_f_bias"][l][4 * hf:4 * hf + 4].reshape(4, 1).copy()
    wk = inp["w_mem_kv"][l]
    d["wkv"] = np.ascontiguousarray(np.concatenate([wk[:, 256 * hf:256 * hf + 256], wk[:, 512 + 256 * hf:512 + 256 * hf + 256]], 1))
    d["mg"] = np.ascontiguousarray(inp["mem_norm_g"][l].reshape(8, 128).T)
    ch = np.concatenate([256 * hf + np.arange(256), 512 + 128 * hf + np.arange(128), 768 + 128 * hf + np.arange(128)])
    d["cw"] = np.ascontiguousarray(inp["ssm_conv_w"][l][:, ch].reshape(4, 4, 128).transpose(2, 1, 0))
    d["cb"] = np.ascontiguousarray(inp["ssm_conv_b"][l][ch].reshape(4, 128).T)
    d["dtb"] = inp["ssm_dt_bias"][l][4 * hf:4 * hf + 4].reshape(4, 1).copy()
    d["alog"] = inp["ssm_a_log"][l][4 * hf:4 * hf + 4].reshape(4, 1).copy()
    d["dsk"] = np.ascontiguousarray(np.broadcast_to(np.repeat(inp["ssm_d"][l][4 * hf:4 * hf + 4], 64)[None, :], (128, 256)))
    d["ngs"] = np.ascontiguousarray(inp["ssm_norm_g"][l][256 * hf:256 * hf + 256].reshape(2, 128).T)
    d.update(nsa_host(S, inp["rel_bias_table"][:, 4 * hf:4 * hf + 4], inp["nsa_cmp_pe"][l], inp["nsa_cmp_w1"][l], inp["nsa_cmp_w2"][l]))
    return {pfx + k: v for k, v in d.items()}


def mixers(P, C, S, xT, memT, L, scr, OT):
    phase_inproj(P, C, S, xT, L["wl"], L["gl"], scr["PFb"], scr["PFf"], scr["VT"])
    phase_fox(P, C, S, scr["PFb"], scr["PFf"], scr["VT"], L["fb"], OT)
    phase_mem(P, C, S, scr["PFb"], memT, L["wkv"], L["mg"], OT)
    phase_ssd(P, C, S, scr["PFb"], scr["PFf"], L["cw"], L["cb"], L["dtb"], L["alog"], L["dsk"], L["ngs"], OT)
    phase_nsa(P, C, S, scr["PFb"], scr["PFf"], scr["VT"], L, scr["GALLd"], scr["GLd"], OT)


def alloc_scratch(nc, S, kind="Internal"):
    dt = lambda n, s, d: nc.dram_tensor(n, s, d, kind=kind).ap()
    LALL = S + 4080 + 383 + 767
    return {"PFb": dt("PFb", [18 * 128, S], BF16), "PFf": dt("PFf", [5 * 128, S], F32), "VT": dt("VT", [S, 6, 128], BF16),
            "GALLd": dt("GALLd", [4, LALL], F32), "GLd": dt("GLd", [12, S], F32)}


SEQ = 4096
NCORES = 8


def _dt(nc):
    return lambda n, s, d, k="ExternalInput": nc.dram_tensor(n, s, d, kind=k).ap()


def build_L1(S):
    nc = bass.Bass("TRN2", target_bir_lowering=False)
    dt = _dt(nc)
    xT = dt("xT", [1024, S], F32)
    memT = dt("memT", [1024, 256], F32)
    cd = {n: dt(n, s, d) for n, s, d in CONST_SPECS}
    L = {n: dt(n, s, d) for n, s, d in layer_specs(S, "")}
    scr = alloc_scratch(nc, S)
    OT = dt("OT", [1024, S], BF16, "ExternalOutput")
    P = Prog(nc)
    C = setup_ctx2(P, nc, cd)
    mixers(P, C, S, xT, memT, L, scr, OT)
    P.wait_all("sp")
    P.emit()
    P.close()
    return nc


def build_L2(S):
    nc = bass.Bass("TRN2", target_bir_lowering=False)
    dt = _dt(nc)
    xT = dt("xT", [1024, S], F32)
    memT = dt("memT", [1024, 256], F32)
    OTf = dt("OTf", [2048, S], BF16)
    wo = dt("wo", [2048, 1024], F32)
    cd = {n: dt(n, s, d) for n, s, d in CONST_SPECS}
    L = {n: dt(n, s, d) for n, s, d in layer_specs(S, "")}
    scr = alloc_scratch(nc, S)
    x1T = dt("x1T", [1024, S], F32, "ExternalOutput")
    OT = dt("OT", [1024, S], BF16, "ExternalOutput")
    P = Prog(nc)
    C = setup_ctx2(P, nc, cd)
    phase_outproj(P, C, S, xT, OTf, wo, x1T)
    mixers(P, C, S, x1T, memT, L, scr, OT)
    P.wait_all("sp")
    P.emit()
    P.close()
    return nc


def build_L3(St):
    nc = bass.Bass("TRN2", target_bir_lowering=False)
    dt = _dt(nc)
    xT = dt("xT", [1024, St], F32)
    OTf = dt("OTf", [2048, St], BF16)
    wo = dt("wo", [2048, 1024], F32)
    fg = dt("fg", [128, 8], F32)
    cd = {n: dt(n, s, d) for n, s, d in CONST_SPECS}
    x2T = nc.dram_tensor("x2T", [1024, St], F32, kind="Internal").ap()
    outT = dt("outT", [1024, St], F32, "ExternalOutput")
    P = Prog(nc)
    C = setup_ctx2(P, nc, cd)
    phase_outproj(P, C, St, xT, OTf, wo, x2T)
    phase_final(P, C, St, x2T, fg, outT)
    P.wait_all("sp")
    P.emit()
    P.close()
    return nc


def wo_perm(w):
    idx = np.concatenate([512 * m + 256 * hf + np.arange(256) for hf in range(2) for m in range(4)])
    return np.ascontiguousarray(w[idx])


def build_fused(S):
    nc = bass.Bass("TRN2", target_bir_lowering=False)
    dt = _dt(nc)
    xT = dt("xT", [1024, S], F32)
    memT = dt("memT", [1024, 256], F32)
    fg = dt("fg", [128, 8], F32)
    cd = {n: dt(n, s, d) for n, s, d in CONST_SPECS}
    LL = {}
    for l in range(2):
        for hf in range(2):
            pfx = "l%dh%d_" % (l, hf)
            LL[(l, hf)] = {n[len(pfx):]: dt(n, s, d) for n, s, d in layer_specs(S, pfx)}
    wos = [dt("wo%d" % l, [2048, 1024], F32) for l in range(2)]
    scr = alloc_scratch(nc, S)
    OTf = nc.dram_tensor("OTf", [2048, S], BF16, kind="Internal").ap()
    xa = nc.dram_tensor("xa", [1024, S], F32, kind="Internal").ap()
    xb = nc.dram_tensor("xb", [1024, S], F32, kind="Internal").ap()
    outT = dt("outT", [1024, S], F32, "ExternalOutput")
    P = Prog(nc)
    C = setup_ctx2(P, nc, cd)
    xin = xT
    for l in range(2):
        for hf in range(2):
            mixers(P, C, S, xin, memT, LL[(l, hf)], scr, OTf[hf * 1024:(hf + 1) * 1024, :])
        xout = xa if l == 0 else xb
        phase_outproj(P, C, S, xin, OTf, wos[l], xout)
        xin = xout
    phase_final(P, C, S, xin, fg, outT)
    P.wait_all("sp")
    P.emit()
    P.close()
    return nc


def kernel(**inputs):
    inp = {k: np.asarray(v) for k, v in inputs.items()}
    S = inp["x"].shape[1]
    B = inp["x"].shape[0]
    consts = make_consts2()
    fgh = np.ascontiguousarray(inp["final_norm_g"].reshape(8, 128).T)
    shared = dict(consts)
    shared["fg"] = fgh
    for l in range(2):
        shared["wo%d" % l] = wo_perm(inp["w_out"][l])
        for hf in range(2):
            shared.update(layer_host(inp, l, hf, S, "l%dh%d_" % (l, hf)))
    ins = []
    for b in range(B):
        d = {"xT": np.ascontiguousarray(inp["x"][b].T), "memT": np.ascontiguousarray(inp["mem"][b].T)}
        d.update(shared)
        ins.append(d)
    res = run_bass_kernel_spmd(build_fused(S), ins, core_ids=list(range(B))).results
    out = np.empty((B, S, 1024), np.float32)
    for b in range(B):
        out[b] = np.asarray(res[b]["outT"]).T
    return out
```
